# Optimizing a Trainium2 kernel written in Bass

```python
import math
import jax, jax.numpy as jnp
from jax import lax
import numpy as np

D_MODEL = 1024
BATCH = 8
SEQ = 4096
DEPTH = 2

N_MEM = 256
N_MIXERS = 2
D_MIX = D_MODEL
XA_HEADS = 4
XA_HEAD_DIM = D_MODEL // 16
D_XA = XA_HEADS * XA_HEAD_DIM
D_TOK = D_MIX - D_XA
CONV_WIDTH = 3
ML_HEADS = 4
ML_HEAD_DIM = D_TOK // ML_HEADS
ML_CHUNK = 64
QK_CONV_WIDTH = 4
D_FF = 256 * int(math.ceil(8 * D_MODEL / 3 / 256))
LN_EPS = 1e-5
DEEPNORM_ALPHA = (2.0 * DEPTH) ** 0.25
DEEPNORM_BETA = (8.0 * DEPTH) ** -0.25
N_CONV_LAYERS = (DEPTH + 1) // 2
N_MLSTM_LAYERS = DEPTH // 2
D_IN_CONV = 3 * D_TOK + D_XA
D_IN_MLSTM = 4 * D_TOK + 2 * ML_HEADS + D_XA

kernel_name = "hybrid_shortconv_mlstm_memxattn_macaron_deepnorm"


def layer_norm(x, g, b):
    xf = x.astype(jnp.float32)
    mu = jnp.mean(xf, -1, keepdims=True)
    var = jnp.mean(jnp.square(xf - mu), -1, keepdims=True)
    y = (xf - mu) * lax.rsqrt(var + LN_EPS)
    return (y * g.astype(jnp.float32) + b.astype(jnp.float32)).astype(x.dtype)


def swiglu(x, w_gate, w_up, w_down):
    return (jax.nn.silu(x @ w_gate) * (x @ w_up)) @ w_down


def causal_dwconv(u, w):
    width = w.shape[0]
    s = u.shape[1]
    up = jnp.pad(u, ((0, 0), (width - 1, 0), (0, 0)))
    out = up[:, width - 1:width - 1 + s] * w[width - 1]
    for j in range(width - 1):
        out = out + up[:, j:j + s] * w[j]
    return out


def memory_cross_attention(q, mem_kv):
    b, s, _ = q.shape
    q = q.reshape(b, s, XA_HEADS, XA_HEAD_DIM)
    k, v = jnp.split(mem_kv, 2, -1)
    k = k.reshape(b, N_MEM, XA_HEADS, XA_HEAD_DIM)
    v = v.reshape(b, N_MEM, XA_HEADS, XA_HEAD_DIM)
    scores = jnp.einsum('bshd,bmhd->bhsm', q, k).astype(jnp.float32) * (XA_HEAD_DIM ** -0.5)
    p = jax.nn.softmax(scores, -1).astype(v.dtype)
    o = jnp.einsum('bhsm,bmhd->bshd', p, v)
    return o.reshape(b, s, D_XA)


def short_conv_mixer(u, conv_w):
    b_gate, c_gate, x_in = jnp.split(u, 3, -1)
    return b_gate * causal_dwconv(c_gate * x_in, conv_w)


def mlstm_chunkwise(q, k, v, log_i, log_f):
    b, s, _ = q.shape
    h_, dh, L = ML_HEADS, ML_HEAD_DIM, ML_CHUNK
    nc = s // L

    def to_chunks(t):
        return t.astype(jnp.float32).reshape(b, nc, L, h_, dh).transpose(1, 0, 3, 2, 4)

    def gate_chunks(g):
        return g.astype(jnp.float32).reshape(b, nc, L, h_).transpose(1, 0, 3, 2)

    qc, kc, vc = to_chunks(q), to_chunks(k) * (dh ** -0.5), to_chunks(v)
    ic, fc = gate_chunks(log_i), gate_chunks(log_f)
    causal = jnp.tril(jnp.ones((L, L), dtype=bool))

    def step(carry, xs):
        c_st, n_st, m_st = carry
        q_, k_, v_, li, lf = xs
        bcum = jnp.cumsum(lf, -1)
        log_d = bcum[..., :, None] - bcum[..., None, :] + li[..., None, :]
        log_d = jnp.where(causal, log_d, -jnp.inf)
        log_inter = bcum + m_st[..., None]
        m_t = jnp.maximum(log_inter, jnp.max(log_d, -1))
        w_intra = jnp.exp(log_d - m_t[..., None])
        w_inter = jnp.exp(log_inter - m_t)
        sc = jnp.einsum('bhtd,bhsd->bhts', q_, k_) * w_intra
        num = (jnp.einsum('bhts,bhse->bhte', sc, v_)
               + w_inter[..., None] * jnp.einsum('bhtd,bhde->bhte', q_, c_st))
        den = jnp.sum(sc, -1) + w_inter * jnp.einsum('bhtd,bhd->bht', q_, n_st)
        h = num / jnp.maximum(jnp.abs(den), jnp.exp(-m_t))[..., None]
        b_last = bcum[..., -1]
        log_w = b_last[..., None] - bcum + li
        m_new = jnp.maximum(b_last + m_st, jnp.max(log_w, -1))
        w_k = jnp.exp(log_w - m_new[..., None])
        decay = jnp.exp(b_last + m_st - m_new)
        c_new = decay[..., None, None] * c_st + jnp.einsum('bhs,bhsd,bhse->bhde', w_k, k_, v_)
        n_new = decay[..., None] * n_st + jnp.einsum('bhs,bhsd->bhd', w_k, k_)
        return (c_new, n_new, m_new), h

    init = (jnp.zeros((b, h_, dh, dh), jnp.float32),
            jnp.zeros((b, h_, dh), jnp.float32),
            jnp.zeros((b, h_), jnp.float32))
    _, hs = lax.scan(step, init, (qc, kc, vc, ic, fc))
    return hs.transpose(1, 0, 3, 2, 4).reshape(b, s, h_, dh)


def mlstm_mixer(u, b_gates, qk_conv_w, head_norm_g):
    b, s, _ = u.shape
    qk, v, o_pre, gates = jnp.split(u, [2 * D_TOK, 3 * D_TOK, 4 * D_TOK], -1)
    qk = jax.nn.silu(causal_dwconv(qk, qk_conv_w))
    q, k = jnp.split(qk, 2, -1)
    gates = gates.astype(jnp.float32) + b_gates.astype(jnp.float32)
    log_i = gates[..., :ML_HEADS]
    log_f = jax.nn.log_sigmoid(gates[..., ML_HEADS:])
    h = mlstm_chunkwise(q, k, v, log_i, log_f)
    mu = jnp.mean(h, -1, keepdims=True)
    var = jnp.mean(jnp.square(h - mu), -1, keepdims=True)
    h = (h - mu) * lax.rsqrt(var + LN_EPS) * head_norm_g.astype(jnp.float32)
    return jax.nn.sigmoid(o_pre) * h.reshape(b, s, D_TOK).astype(u.dtype)


def setup_inputs(seed: int = 0) -> dict:
    key = jax.random.key(seed)
    ks = jax.random.split(key, 20)
    nrm = jax.random.normal
    f32 = jnp.float32
    x = nrm(ks[0], (BATCH, SEQ, D_MODEL), f32)
    mem = nrm(ks[1], (BATCH, N_MEM, D_MODEL), f32)
    ln_g = 1.0 + 0.02 * nrm(ks[2], (DEPTH, 3, D_MODEL), f32)
    ln_b = 0.02 * nrm(ks[3], (DEPTH, 3, D_MODEL), f32)
    ffn_w_gate = nrm(ks[4], (DEPTH, 2, D_MODEL, D_FF), f32) * D_MODEL ** -0.5
    ffn_w_up = nrm(ks[5], (DEPTH, 2, D_MODEL, D_FF), f32) * D_MODEL ** -0.5
    ffn_w_down = nrm(ks[6], (DEPTH, 2, D_FF, D_MODEL), f32) * (D_FF ** -0.5 * DEEPNORM_BETA)
    w_kv_mem = nrm(ks[7], (DEPTH, D_MODEL, 2 * D_XA), f32) * D_MODEL ** -0.5
    w_out = nrm(ks[8], (DEPTH, D_MIX, D_MODEL), f32) * (D_MIX ** -0.5 * DEEPNORM_BETA)
    w_in_conv = nrm(ks[9], (N_CONV_LAYERS, D_MODEL, D_IN_CONV), f32) * D_MODEL ** -0.5
    conv_w = nrm(ks[10], (N_CONV_LAYERS, CONV_WIDTH, D_TOK), f32) * CONV_WIDTH ** -0.5
    w_in_mlstm = nrm(ks[11], (N_MLSTM_LAYERS, D_MODEL, D_IN_MLSTM), f32) * D_MODEL ** -0.5
    w_in_mlstm = w_in_mlstm.at[:, :, 4 * D_TOK:4 * D_TOK + 2 * ML_HEADS].multiply(0.1)
    b_i = 0.1 * nrm(ks[12], (N_MLSTM_LAYERS, ML_HEADS), f32)
    b_f = jnp.linspace(3.0, 6.0, ML_HEADS, dtype=f32) + 0.1 * nrm(ks[13], (N_MLSTM_LAYERS, ML_HEADS), f32)
    b_gates = jnp.concatenate([b_i, b_f], -1)
    qk_conv_w = nrm(ks[14], (N_MLSTM_LAYERS, QK_CONV_WIDTH, 2 * D_TOK), f32) * QK_CONV_WIDTH ** -0.5
    head_norm_g = 1.0 + 0.02 * nrm(ks[15], (N_MLSTM_LAYERS, ML_HEADS, ML_HEAD_DIM), f32)
    return {"x": x, "mem": mem, "ln_g": ln_g, "ln_b": ln_b,
            "ffn_w_gate": ffn_w_gate, "ffn_w_up": ffn_w_up, "ffn_w_down": ffn_w_down,
            "w_kv_mem": w_kv_mem, "w_out": w_out,
            "w_in_conv": w_in_conv, "conv_w": conv_w,
            "w_in_mlstm": w_in_mlstm, "b_gates": b_gates,
            "qk_conv_w": qk_conv_w, "head_norm_g": head_norm_g}


def reference(x, mem, ln_g, ln_b, ffn_w_gate, ffn_w_up, ffn_w_down, w_kv_mem, w_out,
              w_in_conv, conv_w, w_in_mlstm, b_gates, qk_conv_w, head_norm_g):
    alpha = DEEPNORM_ALPHA
    for l in range(DEPTH):
        x = layer_norm(alpha * x + 0.5 * swiglu(x, ffn_w_gate[l, 0], ffn_w_up[l, 0], ffn_w_down[l, 0]),
                       ln_g[l, 0], ln_b[l, 0])
        mem_kv = mem @ w_kv_mem[l]
        j = l // N_MIXERS
        if l % N_MIXERS == 0:
            u = x @ w_in_conv[j]
            tok = short_conv_mixer(u[..., :3 * D_TOK], conv_w[j])
        else:
            u = x @ w_in_mlstm[j]
            tok = mlstm_mixer(u[..., :4 * D_TOK + 2 * ML_HEADS], b_gates[j], qk_conv_w[j], head_norm_g[j])
        xa = memory_cross_attention(u[..., -D_XA:], mem_kv)
        mix = jnp.concatenate([tok, xa], -1) @ w_out[l]
        x = layer_norm(alpha * x + mix, ln_g[l, 1], ln_b[l, 1])
        x = layer_norm(alpha * x + 0.5 * swiglu(x, ffn_w_gate[l, 1], ffn_w_up[l, 1], ffn_w_down[l, 1]),
                       ln_g[l, 2], ln_b[l, 2])
    return x
```

```python
from contextlib import ExitStack
import numpy as np
import concourse.bass as bass
import concourse.mybir as mybir
from concourse.bass_utils import run_bass_kernel_spmd

F32 = mybir.dt.float32
BF16 = mybir.dt.bfloat16
AF = mybir.ActivationFunctionType
ALU = mybir.AluOpType
AX = mybir.AxisListType

D = 1024
DFF = 2816
NMEM = 256
DTOK = 768
DXA = 256
DH = 192
NH = 4
D_IN_CONV = 2560
D_IN_ML = 3336
EPS = 1e-5
ALPHA = 4.0 ** 0.25
KC = 8
FC = 22

SAME_ENG_SYNC = True


class Sched:
    def __init__(self, nc, es):
        self.nc = nc
        self.es = es
        self.h = {"pe": nc.tensor, "dve": nc.vector, "act": nc.scalar, "pool": nc.gpsimd, "sp": nc.sync}
        self.sem = {k: es.enter_context(nc.semaphore("s_" + k)) for k in self.h}
        self.cnt = {k: 0 for k in self.h}
        self.obs = {k: {} for k in self.h}
        self.lw = {}
        self.rd = {}
        self.dsem = {}

    def semh(self, sk):
        return self.sem[sk] if sk in self.sem else self.dsem[sk][0]

    def _deps(self, eng, reads, writes):
        need = {}

        def add(rec):
            if rec is None:
                return
            sk, v = rec
            if sk in self.dsem:
                v = self.dsem[sk][1]
            if need.get(sk, 0) < v:
                need[sk] = v

        for k in reads:
            add(self.lw.get(k))
        for k in writes:
            add(self.lw.get(k))
            for sk, v in self.rd.get(k, {}).items():
                add((sk, v))
        for sk, v in need.items():
            if self.obs[eng].get(sk, 0) >= v:
                continue
            if sk == eng and (eng == "pe" or eng == "sp" or not SAME_ENG_SYNC):
                continue
            self.obs[eng][sk] = v
            self.h[eng].wait_ge(self.semh(sk), v)

    def _record(self, rec, reads, writes):
        for k in reads:
            d = self.rd.setdefault(k, {})
            if d.get(rec[0], 0) < rec[1]:
                d[rec[0]] = rec[1]
        for k in writes:
            self.lw[k] = rec
            self.rd[k] = {}

    def op(self, eng, fn, reads=(), writes=()):
        self._deps(eng, reads, writes)
        ins = fn(self.h[eng])
        self.cnt[eng] += 1
        ins.then_inc(self.sem[eng], 1)
        self._record((eng, self.cnt[eng]), reads, writes)

    def dma(self, eng, semname, out, in_, reads=(), writes=()):
        if semname not in self.dsem:
            self.dsem[semname] = [self.es.enter_context(self.nc.semaphore("d_" + semname)), 0]
        self._deps(eng, reads, writes)
        ins = self.h[eng].dma_start(out=out, in_=in_)
        d = self.dsem[semname]
        d[1] += 16
        ins.then_inc(d[0], 16)
        self._record((semname, d[1]), reads, writes)

    def barrier(self):
        for e in self.h:
            for o in self.h:
                if o != e and self.cnt[o] > self.obs[e].get(o, 0):
                    self.obs[e][o] = self.cnt[o]
                    self.h[e].wait_ge(self.sem[o], self.cnt[o])
            for name, (hd, c) in self.dsem.items():
                if c > self.obs[e].get(name, 0):
                    self.obs[e][name] = c
                    self.h[e].wait_ge(hd, c)
        self.lw = {}
        self.rd = {}


class Ctx:
    pass


def fm(ap):
    return ap.rearrange("(c p) t -> p c t", p=128)


def layer_norm_inplace(S, cx, X, T, gcol, bcol, key):
    nc = cx.nc
    ps_m, ps_q = cx.ps[6], cx.ps[7]
    zsq, mean, var = cx.ln_zsq, cx.ln_mean, cx.ln_var
    S.op("pool", lambda e: e.tensor_tensor(zsq[:, :, :T], X[:, :, :T], X[:, :, :T], ALU.mult),
         reads=[key], writes=["ln_zsq"])

    def mm_mean(pe):
        for c in range(KC):
            i = pe.matmul(ps_m[:, :T], cx.ones_f[:], X[:, c, :T], start=(c == 0), stop=(c == KC - 1))
        return i
    S.op("pe", mm_mean, reads=[key, "consts"], writes=["ps6"])

    def mm_sq(pe):
        for c in range(KC):
            i = pe.matmul(ps_q[:, :T], cx.ones_f[:], zsq[:, c, :T], start=(c == 0), stop=(c == KC - 1))
        return i
    S.op("pe", mm_sq, reads=["ln_zsq", "consts"], writes=["ps7"])
    S.op("dve", lambda e: e.tensor_copy(mean[:, :T], ps_m[:, :T]), reads=["ps6"], writes=["ln_mean"])
    S.op("dve", lambda e: e.tensor_tensor(var[:, :T], mean[:, :T], mean[:, :T], ALU.mult),
         reads=["ln_mean"], writes=["ln_var"])
    S.op("dve", lambda e: e.tensor_tensor(var[:, :T], ps_q[:, :T], var[:, :T], ALU.subtract),
         reads=["ps7", "ln_var"], writes=["ln_var"])
    S.op("dve", lambda e: e.tensor_scalar(var[:, :T], var[:, :T], EPS, None, ALU.add),
         reads=["ln_var"], writes=["ln_var"])
    S.op("pool", lambda e: e.tensor_tensor(var[:, :T], var[:, :T], cx.mhalf[:, :T], ALU.pow),
         reads=["ln_var", "consts"], writes=["ln_var"])
    mean_b = mean[:, :T].unsqueeze(1).to_broadcast([128, KC, T])
    rstd_b = var[:, :T].unsqueeze(1).to_broadcast([128, KC, T])
    g_b = cx.smalls[:, gcol:gcol + KC].unsqueeze(2).to_broadcast([128, KC, T])
    b_b = cx.smalls[:, bcol:bcol + KC].unsqueeze(2).to_broadcast([128, KC, T])
    S.op("dve", lambda e: e.tensor_tensor(X[:, :, :T], X[:, :, :T], mean_b, ALU.subtract),
         reads=[key, "ln_mean"], writes=[key])
    S.op("pool", lambda e: e.tensor_tensor(X[:, :, :T], X[:, :, :T], rstd_b, ALU.mult),
         reads=[key, "ln_var"], writes=[key])
    S.op("dve", lambda e: e.tensor_tensor(X[:, :, :T], X[:, :, :T], g_b, ALU.mult),
         reads=[key, "consts"], writes=[key])
    S.op("pool", lambda e: e.tensor_tensor(X[:, :, :T], X[:, :, :T], b_b, ALU.add),
         reads=[key, "consts"], writes=[key])


def ffn_stage(S, cx, src, dst, Wg, Wu, Wd, gcol, bcol, NT, T, sid):
    nc = cx.nc
    with ExitStack() as es:
        def sb(name, shape, dt):
            return es.enter_context(nc.sbuf_tensor(f"{name}_{sid}", shape, dt))
        wg = sb("wg", [128, KC, DFF], BF16)
        wu = sb("wu", [128, KC, DFF], BF16)
        wd = sb("wd", [128, FC, D], BF16)
        xs = [sb(f"xs{i}", [128, KC, T], F32) for i in range(2)]
        xb = sb("xb", [128, KC, T], BF16)
        hb = sb("hb", [128, FC, T], BF16)
        sg = [sb(f"sg{i}", [128, T], F32) for i in range(2)]
        HF = DFF // 2
        Wg_v = Wg.rearrange("(kc p) f -> p kc f", p=128)
        Wu_v = Wu.rearrange("(kc p) f -> p kc f", p=128)
        Wd_v = Wd.rearrange("(fc p) d -> p fc d", p=128)
        for hf in range(2):
            S.dma("pool", f"wg{hf}", out=wg[:, :, hf * HF:(hf + 1) * HF], in_=Wg_v[:, :, hf * HF:(hf + 1) * HF],
                  writes=[("wg", hf)])
            S.dma("pool", f"wu{hf}", out=wu[:, :, hf * HF:(hf + 1) * HF], in_=Wu_v[:, :, hf * HF:(hf + 1) * HF],
                  writes=[("wu", hf)])
        for q in range(2):
            S.dma("pool", f"wd{q}", out=wd[:, q * 11:(q + 1) * 11, :], in_=Wd_v[:, q * 11:(q + 1) * 11, :],
                  writes=[("wd", q)])
        src_v, dst_v = fm(src), fm(dst)
        ntile = NT // T

        def load(ti):
            sl = ti % 2
            S.dma("sp", f"xld{sl}", out=xs[sl][:], in_=src_v[:, :, ti * T:(ti + 1) * T],
                  reads=["src"], writes=[("xs", sl)])
        load(0)
        for ti in range(ntile):
            sl = ti % 2
            X = xs[sl]
            xk = ("xs", sl)
            if ti + 1 < ntile:
                load(ti + 1)
            S.op("act", lambda e: e.activation(xb[:], X[:], AF.Copy), reads=[xk], writes=["xb"])
            S.op("pool", lambda e: e.tensor_scalar(X[:], X[:], ALPHA, None, ALU.mult), reads=[xk], writes=[xk])
            for fc in range(FC):
                hf = fc // 11
                pg, pu = cx.ps[fc % 2], cx.ps[2 + fc % 2]

                def mm_g(pe, fc=fc, pg=pg):
                    for c in range(KC):
                        i = pe.matmul(pg[:, :T], wg[:, c, fc * 128:(fc + 1) * 128], xb[:, c, :],
                                      start=(c == 0), stop=(c == KC - 1))
                    return i

                def mm_u(pe, fc=fc, pu=pu):
                    for c in range(KC):
                        i = pe.matmul(pu[:, :T], wu[:, c, fc * 128:(fc + 1) * 128], xb[:, c, :],
                                      start=(c == 0), stop=(c == KC - 1))
                    return i
                S.op("pe", mm_g, reads=[("wg", hf), "xb"], writes=[f"ps{fc % 2}"])
                S.op("pe", mm_u, reads=[("wu", hf), "xb"], writes=[f"ps{2 + fc % 2}"])
                sgt = sg[fc % 2]
                S.op("act", lambda e, pg=pg, sgt=sgt: e.activation(sgt[:], pg[:, :T], AF.Silu),
                     reads=[f"ps{fc % 2}"], writes=[("sg", fc % 2)])
                S.op("dve", lambda e, pu=pu, sgt=sgt, fc=fc: e.tensor_tensor(hb[:, fc, :], sgt[:], pu[:, :T], ALU.mult),
                     reads=[("sg", fc % 2), f"ps{2 + fc % 2}"], writes=[("hb", fc)])
            for dc in range(KC):
                py = cx.ps[4 + dc % 2]

                def mm_d(pe, dc=dc, py=py):
                    for f in range(FC):
                        i = pe.matmul(py[:, :T], wd[:, f, dc * 128:(dc + 1) * 128], hb[:, f, :],
                                      start=(f == 0), stop=(f == FC - 1))
                    return i
                S.op("pe", mm_d, reads=[("wd", 0), ("wd", 1)] + [("hb", f) for f in range(FC)],
                     writes=[f"ps{4 + dc % 2}"])
                S.op("dve", lambda e, dc=dc, py=py: e.scalar_tensor_tensor(
                    out=X[:, dc, :], in0=py[:, :T], scalar=0.5, in1=X[:, dc, :], op0=ALU.mult, op1=ALU.add),
                    reads=[f"ps{4 + dc % 2}", xk], writes=[xk])
            layer_norm_inplace(S, cx, X, T, gcol, bcol, xk)
            S.dma("sp", f"xst{sl}", out=dst_v[:, :, ti * T:(ti + 1) * T], in_=X[:],
                  reads=[xk], writes=["dst"])
        S.barrier()


def load_cast(S, sem, out, in_, key, nsplit=1, axis_len=None):
    S.dma("pool", sem, out=out, in_=in_, writes=[key])


def kv_precompute(S, cx, sb, memT, Wkv):
    nc = cx.nc
    memb = sb("memb", [128, KC, NMEM], BF16)
    wkv = sb("wkv", [128, KC, 2 * DXA], BF16)
    kT = sb("kT", [64, NH, NMEM], BF16)
    vpad = sb("vpad", [128, 2, NH, 128], BF16)
    onespad = sb("onespad", [128, 2, 128], BF16)
    S.dma("pool", "memb", out=memb[:], in_=memT.rearrange("(c p) m -> p c m", p=128), writes=["memb"])
    S.dma("pool", "wkv", out=wkv[:], in_=Wkv.rearrange("(c p) n -> p c n", p=128), writes=["wkv"])
    S.op("dve", lambda e: e.memset(vpad[:], 0.0), writes=["vpad"])
    S.op("dve", lambda e: e.memset(onespad[:], 0.0), writes=["onespad"])
    S.op("dve", lambda e: e.memset(onespad[:, 0, 0:64], 1.0), writes=["onespad"])
    S.op("dve", lambda e: e.memset(onespad[:, 1, 64:128], 1.0), writes=["onespad"])
    for h in range(NH):
        ps = cx.ps[h % 2]

        def mm(pe, h=h, ps=ps):
            for c in range(KC):
                i = pe.matmul(ps[0:64, :NMEM], wkv[:, c, 64 * h:64 * h + 64], memb[:, c, :],
                              start=(c == 0), stop=(c == KC - 1))
            return i
        S.op("pe", mm, reads=["memb", "wkv"], writes=[f"ps{h % 2}"])
        S.op("dve", lambda e, h=h, ps=ps: e.tensor_copy(kT[:, h, :], ps[0:64, :NMEM]),
             reads=[f"ps{h % 2}"], writes=["kT"])
    for mc in range(2):
        ps = cx.ps[2 + mc]

        def mm(pe, mc=mc, ps=ps):
            for c in range(KC):
                i = pe.matmul(ps[:, :DXA], memb[:, c, mc * 128:(mc + 1) * 128], wkv[:, c, DXA:2 * DXA],
                              start=(c == 0), stop=(c == KC - 1))
            return i
        S.op("pe", mm, reads=["memb", "wkv"], writes=[f"ps{2 + mc}"])
        for h in range(NH):
            o = 64 * (h % 2)
            S.op("dve", lambda e, h=h, mc=mc, ps=ps, o=o: e.tensor_copy(vpad[:, mc, h, o:o + 64], ps[:, 64 * h:64 * h + 64]),
                 reads=[f"ps{2 + mc}"], writes=["vpad"])
    return kT, vpad, onespad


def xattn_out_ln(S, cx, X, xk, xb, win, qcol, wout, mixin, qT, ebuf, rden, kv, T, gcol, bcol):
    kT, vpad, onespad = kv
    for h in range(NH):
        ps = cx.ps[h % 2]

        def mm(pe, h=h, ps=ps):
            for c in range(KC):
                i = pe.matmul(ps[0:64, :T], win[:, c, qcol + 64 * h:qcol + 64 * h + 64], xb[:, c, :T],
                              start=(c == 0), stop=(c == KC - 1))
            return i
        S.op("pe", mm, reads=["win", "xb"], writes=[f"ps{h % 2}"])
        S.op("dve", lambda e, h=h, ps=ps: e.tensor_copy(qT[:, h, :T], ps[0:64, :T]),
             reads=[f"ps{h % 2}"], writes=[("qT", h)])
    for h in range(NH):
        for mc in range(2):
            i8 = h * 2 + mc
            ps = cx.ps[2 + i8 % 2]
            S.op("pe", lambda pe, h=h, mc=mc, ps=ps: pe.matmul(ps[:, :T], kT[:, h, mc * 128:(mc + 1) * 128], qT[:, h, :T],
                                                             start=True, stop=True),
                 reads=["kT", ("qT", h)], writes=[f"ps{2 + i8 % 2}"])
            S.op("act", lambda e, i8=i8, ps=ps: e.activation(ebuf[:, i8, :T], ps[:, :T], AF.Exp, scale=0.125),
                 reads=[f"ps{2 + i8 % 2}"], writes=[("e", i8)])
    for pr in range(2):
        po, pd = cx.ps[4], cx.ps[5]

        def mm_o(pe, pr=pr):
            n = 0
            for h in (2 * pr, 2 * pr + 1):
                for mc in range(2):
                    i = pe.matmul(po[:, :T], vpad[:, mc, h, :], ebuf[:, h * 2 + mc, :T], start=(n == 0), stop=(n == 3))
                    n += 1
            return i

        def mm_d(pe, pr=pr):
            n = 0
            for h in (2 * pr, 2 * pr + 1):
                for mc in range(2):
                    i = pe.matmul(pd[:, :T], onespad[:, h % 2, :], ebuf[:, h * 2 + mc, :T], start=(n == 0), stop=(n == 3))
                    n += 1
            return i
        er = [("e", i) for i in range(4 * pr, 4 * pr + 4)]
        S.op("pe", mm_o, reads=["vpad"] + er, writes=["ps4"])
        S.op("pe", mm_d, reads=["onespad"] + er, writes=["ps5"])
        S.op("dve", lambda e: e.reciprocal(rden[:, :T], pd[:, :T]), reads=["ps5"], writes=["rden"])
        S.op("dve", lambda e, pr=pr: e.tensor_tensor(mixin[:, 6 + pr, :T], po[:, :T], rden[:, :T], ALU.mult),
             reads=["ps4", "rden"], writes=[("mixin", 6 + pr)])
    for dc in range(KC):
        py = cx.ps[dc % 2]

        def mm_y(pe, dc=dc, py=py):
            for f in range(KC):
                i = pe.matmul(py[:, :T], wout[:, f, dc * 128:(dc + 1) * 128], mixin[:, f, :T],
                              start=(f == 0), stop=(f == KC - 1))
            return i
        S.op("pe", mm_y, reads=["wout"] + [("mixin", f) for f in range(KC)], writes=[f"ps{dc % 2}"])
        S.op("dve", lambda e, dc=dc, py=py: e.scalar_tensor_tensor(
            out=X[:, dc, :T], in0=X[:, dc, :T], scalar=ALPHA, in1=py[:, :T], op0=ALU.mult, op1=ALU.add),
            reads=[f"ps{dc % 2}", xk], writes=[xk])
    layer_norm_inplace(S, cx, X, T, gcol, bcol, xk)


def mix0_stage(S, cx, src, dst, Win, Wout, Wkv, memT, gcol, bcol, NT, T, sid):
    nc = cx.nc
    with ExitStack() as es:
        def sb(name, shape, dt):
            return es.enter_context(nc.sbuf_tensor(f"{name}_{sid}", shape, dt))
        win = sb("win", [128, KC, D_IN_CONV], BF16)
        wout = sb("wout", [128, KC, D], BF16)
        xs = [sb(f"xs{i}", [128, KC, T], F32) for i in range(2)]
        xb = sb("xb", [128, KC, T], BF16)
        mixin = sb("mixin", [128, KC, T], BF16)
        cxb = sb("cxb", [128, 6, T + 2], F32)
        csb = [sb(f"csb{i}", [128, T], F32) for i in range(2)]
        acc = [sb(f"acc{i}", [128, T], F32) for i in range(2)]
        qT = sb("qT", [64, NH, T], BF16)
        ebuf = sb("ebuf", [128, 8, T], BF16)
        rden = sb("rden", [128, T], F32)
        Win_v = Win.rearrange("(kc p) f -> p kc f", p=128)
        for q in range(4):
            S.dma("pool", "win", out=win[:, :, q * 640:(q + 1) * 640], in_=Win_v[:, :, q * 640:(q + 1) * 640],
                  writes=["win"])
        S.dma("pool", "wout", out=wout[:], in_=Wout.rearrange("(kc p) f -> p kc f", p=128), writes=["wout"])
        kv = kv_precompute(S, cx, sb, memT, Wkv)
        S.op("dve", lambda e: e.memset(cxb[:], 0.0), writes=[("cxb", j) for j in range(6)])
        src_v, dst_v = fm(src), fm(dst)
        ntile = NT // T
        CW = 96

        def load(ti):
            sl = ti % 2
            S.dma("sp", f"xld{sl}", out=xs[sl][:], in_=src_v[:, :, ti * T:(ti + 1) * T],
                  reads=["src"], writes=[("xs", sl)])
        load(0)
        for ti in range(ntile):
            sl = ti % 2
            X = xs[sl]
            xk = ("xs", sl)
            if ti + 1 < ntile:
                load(ti + 1)
            S.op("act", lambda e: e.activation(xb[:], X[:], AF.Copy), reads=[xk], writes=["xb"])
            for j in range(6):
                b3 = 3 * (j % 2)
                pc, px, pb = cx.ps[b3], cx.ps[b3 + 1], cx.ps[b3 + 2]

                def mm(pe, ps, col):
                    for c in range(KC):
                        i = pe.matmul(ps[:, :T], win[:, c, col:col + 128], xb[:, c, :],
                                      start=(c == 0), stop=(c == KC - 1))
                    return i
                S.op("pe", lambda pe, pc=pc, j=j: mm(pe, pc, DTOK + j * 128), reads=["win", "xb"], writes=[f"ps{b3}"])
                S.op("pe", lambda pe, px=px, j=j: mm(pe, px, 2 * DTOK + j * 128), reads=["win", "xb"], writes=[f"ps{b3 + 1}"])
                S.op("pe", lambda pe, pb=pb, j=j: mm(pe, pb, j * 128), reads=["win", "xb"], writes=[f"ps{b3 + 2}"])
                cs, ac = csb[j % 2], acc[j % 2]
                S.op("act", lambda e, cs=cs, pc=pc: e.activation(cs[:], pc[:, :T], AF.Copy),
                     reads=[f"ps{b3}"], writes=[("csb", j % 2)])
                S.op("dve", lambda e, j=j, cs=cs, px=px: e.tensor_tensor(cxb[:, j, 2:2 + T], cs[:], px[:, :T], ALU.mult),
                     reads=[("csb", j % 2), f"ps{b3 + 1}"], writes=[("cxb", j)])
                w0 = cx.smalls[:, CW + 0 * 6 + j:CW + 0 * 6 + j + 1]
                w1 = cx.smalls[:, CW + 1 * 6 + j:CW + 1 * 6 + j + 1]
                w2 = cx.smalls[:, CW + 2 * 6 + j:CW + 2 * 6 + j + 1]
                S.op("pool", lambda e, j=j, ac=ac, w2=w2: e.tensor_scalar(ac[:], cxb[:, j, 2:2 + T], w2, None, ALU.mult),
                     reads=[("cxb", j), "consts"], writes=[("acc", j % 2)])
                S.op("dve", lambda e, j=j, ac=ac, w1=w1: e.scalar_tensor_tensor(
                    out=ac[:], in0=cxb[:, j, 1:1 + T], scalar=w1, in1=ac[:], op0=ALU.mult, op1=ALU.add),
                    reads=[("cxb", j), ("acc", j % 2)], writes=[("acc", j % 2)])
                S.op("dve", lambda e, j=j, ac=ac, w0=w0: e.scalar_tensor_tensor(
                    out=ac[:], in0=cxb[:, j, 0:T], scalar=w0, in1=ac[:], op0=ALU.mult, op1=ALU.add),
                    reads=[("cxb", j), ("acc", j % 2)], writes=[("acc", j % 2)])
                S.op("dve", lambda e, j=j, ac=ac, pb=pb: e.tensor_tensor(mixin[:, j, :], ac[:], pb[:, :T], ALU.mult),
                     reads=[("acc", j % 2), f"ps{b3 + 2}"], writes=[("mixin", j)])
                S.op("pool", lambda e, j=j: e.tensor_copy(cxb[:, j, 0:2], cxb[:, j, T:T + 2]),
                     reads=[("cxb", j)], writes=[("cxb", j)])
            xattn_out_ln(S, cx, X, xk, xb, win, 3 * DTOK, wout, mixin, qT, ebuf, rden, kv, T, gcol, bcol)
            S.dma("sp", f"xst{sl}", out=dst_v[:, :, ti * T:(ti + 1) * T], in_=X[:],
                  reads=[xk], writes=["dst"])
        S.barrier()


LOG_QSCALE = float(np.log(DH ** -0.5))
QKW = 120
GB = 184


def mix1_stage(S, cx, src, dst, Win, Wout, Wkv, memT, gcol, bcol, NT, T, sid):
    nc = cx.nc
    NB = T // 128
    with ExitStack() as es:
        def sb(name, shape, dt):
            return es.enter_context(nc.sbuf_tensor(f"{name}_{sid}", shape, dt))
        win = sb("win", [128, KC, D_IN_ML], BF16)
        wout = sb("wout", [128, KC, D], BF16)
        xs = [sb(f"xs{i}", [128, KC, T], F32) for i in range(2)]
        xb = sb("xb", [128, KC, T], BF16)
        mixin = sb("mixin", [128, KC, T], BF16)
        qT = sb("qT", [64, NH, T], BF16)
        ebuf = sb("ebuf", [128, 8, T], BF16)
        rden = sb("rden", [128, T], F32)
        raw = sb("raw", [96, 2, T + 3], F32)
        carry = sb("carry", [96, 16, 3], F32)
        cacc = sb("cacc", [96, 2, T], F32)
        qkT = sb("qkT", [96, 16, T], BF16)
        vaug = sb("vaug", [128, NB, NH, DH + 1], BF16)
        sigo = sb("sigo", [128, 6, T], F32)
        g_i = sb("g_i", [4, T], F32)
        g_x = sb("g_x", [4, T], F32)
        g_t = sb("g_t", [4, T], F32)
        g_f = sb("g_f", [4, T], F32)
        g_F = sb("g_F", [4, T], F32)
        g_a = sb("g_a", [4, T], F32)
        g_M = sb("g_M", [4, T], F32)
        g_al = sb("g_al", [4, T], F32)
        g_fl = sb("g_fl", [4, T], F32)
        g_one = sb("g_one", [4, T], F32)
        Fc = sb("Fc", [4, 1], F32)
        Mc = sb("Mc", [4, 1], F32)
        Mprev = sb("Mprev", [4, NB], F32)
        dec = sb("dec", [4, NB], F32)
        decx = sb("decx", [4, NB, 4], F32)
        tokmaj = sb("tokmaj", [128, NB, 8], F32)
        decb = sb("decb", [96, NB * 4], F32)
        Cst = sb("Cst", [96, 8, DH + 1], F32)
        Cbf = sb("Cbf", [96, 8, DH + 1], BF16)
        scT = sb("scT", [128, NH, 128], BF16)
        kpr = sb("kpr", [128, NH, DH], BF16)
        hx = sb("hx", [128, NH, DH], F32)
        hn = sb("hn", [128, NH * DH], BF16)
        dn = sb("dn", [128, NH], F32)
        bst = sb("bst", [128, NH, 6], F32)
        mv = sb("mv", [128, NH, 2], F32)
        rs4 = sb("rs4", [128, NH], F32)

        Win_v = Win.rearrange("(kc p) f -> p kc f", p=128)
        for q in range(4):
            S.dma("pool", "win", out=win[:, :, q * 834:(q + 1) * 834], in_=Win_v[:, :, q * 834:(q + 1) * 834],
                  writes=["win"])
        S.dma("pool", "wout", out=wout[:], in_=Wout.rearrange("(kc p) f -> p kc f", p=128), writes=["wout"])
        kv = kv_precompute(S, cx, sb, memT, Wkv)
        S.op("dve", lambda e: e.memset(carry[:], 0.0), writes=["carry"])
        S.op("dve", lambda e: e.memset(vaug[:], 1.0), writes=["vaug"])
        S.op("dve", lambda e: e.memset(Cst[:], 0.0), writes=["Cst"])
        S.op("dve", lambda e: e.memset(Fc[:], 0.0), writes=["Fc"])
        S.op("dve", lambda e: e.memset(Mc[:], 0.0), writes=["Mc"])
        S.op("dve", lambda e: e.memset(g_one[:], 1.0), writes=["g_one"])
        psb = cx.ps[5][:].bitcast(BF16)
        src_v, dst_v = fm(src), fm(dst)
        ntile = NT // T

        def load(ti):
            sl = ti % 2
            S.dma("sp", f"xld{sl}", out=xs[sl][:], in_=src_v[:, :, ti * T:(ti + 1) * T],
                  reads=["src"], writes=[("xs", sl)])
        load(0)
        for ti in range(ntile):
            sl = ti % 2
            X = xs[sl]
            xk = ("xs", sl)
            if ti + 1 < ntile:
                load(ti + 1)
            S.op("act", lambda e: e.activation(xb[:], X[:], AF.Copy), reads=[xk], writes=["xb"])

            def proj(ps, pkey, col, m, rows=None):
                def mm(pe):
                    for c in range(KC):
                        i = pe.matmul(ps[0:m, :T], win[:, c, col:col + m], xb[:, c, :],
                                      start=(c == 0), stop=(c == KC - 1))
                    return i
                S.op("pe", mm, reads=["win", "xb"], writes=[pkey])
            for g in range(16):
                qk, h, half = g // 8, (g % 8) // 2, g % 2
                col = qk * DTOK + h * DH + half * 96
                r = g % 2
                ps = cx.ps[r]
                proj(ps, f"ps{r}", col, 96)
                S.op("act", lambda e, r=r, ps=ps: e.activation(raw[:, r, 3:3 + T], ps[0:96, :T], AF.Copy),
                     reads=[f"ps{r}"], writes=[("raw", r)])
                S.op("pool", lambda e, r=r, g=g: e.tensor_copy(raw[:, r, 0:3], carry[:, g, :]),
                     reads=["carry"], writes=[("raw", r)])
                wt = [cx.smalls[0:96, QKW + tap * 16 + g:QKW + tap * 16 + g + 1] for tap in range(4)]
                S.op("pool", lambda e, r=r, wt=wt: e.tensor_scalar(cacc[:, r, :], raw[:, r, 3:3 + T], wt[3], None, ALU.mult),
                     reads=[("raw", r), "consts"], writes=[("cacc", r)])
                for tap in (2, 1, 0):
                    S.op("dve", lambda e, r=r, wt=wt, tap=tap: e.scalar_tensor_tensor(
                        out=cacc[:, r, :], in0=raw[:, r, tap:tap + T], scalar=wt[tap], in1=cacc[:, r, :],
                        op0=ALU.mult, op1=ALU.add),
                        reads=[("raw", r), ("cacc", r), "consts"], writes=[("cacc", r)])
                S.op("pool", lambda e, r=r, g=g: e.tensor_copy(carry[:, g, :], raw[:, r, T:T + 3]),
                     reads=[("raw", r)], writes=["carry"])
                S.op("act", lambda e, r=r, g=g: e.activation(qkT[:, g, :], cacc[:, r, :], AF.Silu),
                     reads=[("cacc", r)], writes=[("qkT", g)])
            for blk in range(NB):
                for hp in range(2):
                    ps = cx.ps[hp]

                    def mm(pe, blk=blk, hp=hp, ps=ps):
                        for c in range(KC):
                            i = pe.matmul(ps[:, :2 * DH], xb[:, c, blk * 128:(blk + 1) * 128],
                                          win[:, c, 2 * DTOK + hp * 2 * DH:2 * DTOK + (hp + 1) * 2 * DH],
                                          start=(c == 0), stop=(c == KC - 1))
                        return i
                    S.op("pe", mm, reads=["win", "xb"], writes=[f"ps{hp}"])
                    S.op("dve", lambda e, blk=blk, hp=hp, ps=ps: e.tensor_copy(
                        vaug[:, blk, 2 * hp:2 * hp + 2, 0:DH], ps[:, :2 * DH].rearrange("p (h e) -> p h e", h=2)),
                        reads=[f"ps{hp}"], writes=["vaug"])
            for j in range(6):
                ps = cx.ps[j % 2]
                proj(ps, f"ps{j % 2}", 3 * DTOK + j * 128, 128)
                S.op("act", lambda e, j=j, ps=ps: e.activation(sigo[:, j, :], ps[:, :T], AF.Sigmoid),
                     reads=[f"ps{j % 2}"], writes=[("sigo", j)])
            proj(cx.ps[0], "ps0", 4 * DTOK, 4)
            proj(cx.ps[1], "ps1", 4 * DTOK + 4, 4)
            S.op("dve", lambda e: e.tensor_scalar(g_i[:], cx.ps[0][0:4, :T], cx.smalls[0:4, GB:GB + 1], None, ALU.add),
                 reads=["ps0", "consts"], writes=["g_i"])
            S.op("dve", lambda e: e.tensor_scalar(g_x[:], cx.ps[1][0:4, :T], cx.smalls[0:4, GB + 1:GB + 2], None, ALU.add),
                 reads=["ps1", "consts"], writes=["g_x"])
            S.op("dve", lambda e: e.tensor_scalar(g_t[:], g_x[:], -1.0, None, ALU.mult), reads=["g_x"], writes=["g_t"])
            S.op("dve", lambda e: e.tensor_tensor(g_t[:], g_t[:], g_x[:], ALU.max), reads=["g_x", "g_t"], writes=["g_t"])
            S.op("act", lambda e: e.activation(g_t[:], g_t[:], AF.Exp, scale=-1.0), reads=["g_t"], writes=["g_t"])
            S.op("act", lambda e: e.activation(g_t[:], g_t[:], AF.Ln, bias=1.0), reads=["g_t"], writes=["g_t"])
            S.op("dve", lambda e: e.tensor_scalar(g_f[:], g_x[:], 0.0, None, ALU.min), reads=["g_x"], writes=["g_f"])
            S.op("dve", lambda e: e.tensor_tensor(g_f[:], g_f[:], g_t[:], ALU.subtract), reads=["g_f", "g_t"], writes=["g_f"])
            S.op("dve", lambda e: e.tensor_tensor_scan(g_F[:], g_one[:], g_f[:], Fc[:, 0:1], ALU.mult, ALU.add),
                 reads=["g_f", "g_one", "Fc"], writes=["g_F"])
            S.op("dve", lambda e: e.tensor_tensor(g_a[:], g_i[:], g_F[:], ALU.subtract), reads=["g_i", "g_F"], writes=["g_a"])
            S.op("dve", lambda e: e.tensor_tensor_scan(g_M[:], g_a[:], g_a[:], Mc[:, 0:1], ALU.max, ALU.max),
                 reads=["g_a", "Mc"], writes=["g_M"])
            Mend = g_M[:].rearrange("p (b t) -> p b t", t=128)[:, :, 127:128]
            Mend_b = Mend.to_broadcast([4, NB, 128])
            S.op("dve", lambda e: e.tensor_copy(Mprev[:, 0:1], Mc[:, 0:1]), reads=["Mc"], writes=["Mprev"])
            if NB > 1:
                S.op("dve", lambda e: e.tensor_copy(Mprev[:, 1:NB].unsqueeze(2), Mend[:, 0:NB - 1, :]),
                     reads=["g_M"], writes=["Mprev"])
            S.op("dve", lambda e: e.tensor_tensor(g_al[:].rearrange("p (b t) -> p b t", t=128),
                                                  g_a[:].rearrange("p (b t) -> p b t", t=128), Mend_b, ALU.subtract),
                 reads=["g_a", "g_M"], writes=["g_al"])
            S.op("dve", lambda e: e.tensor_scalar(g_al[:], g_al[:], LOG_QSCALE, None, ALU.add), reads=["g_al"], writes=["g_al"])
            S.op("act", lambda e: e.activation(g_al[:], g_al[:], AF.Exp), reads=["g_al"], writes=["g_al"])
            S.op("dve", lambda e: e.tensor_tensor(g_fl[:].rearrange("p (b t) -> p b t", t=128),
                                                  g_F[:].rearrange("p (b t) -> p b t", t=128), Mend_b, ALU.add),
                 reads=["g_F", "g_M"], writes=["g_fl"])
            S.op("act", lambda e: e.activation(g_fl[:], g_fl[:], AF.Exp, scale=-1.0), reads=["g_fl"], writes=["g_fl"])
            S.op("dve", lambda e: e.tensor_tensor(dec[:].unsqueeze(2), Mprev[:].unsqueeze(2), Mend, ALU.subtract),
                 reads=["Mprev", "g_M"], writes=["dec"])
            S.op("act", lambda e: e.activation(dec[:], dec[:], AF.Exp), reads=["dec"], writes=["dec"])
            S.op("dve", lambda e: e.tensor_tensor(decx[:], dec[:].unsqueeze(2).to_broadcast([4, NB, 4]),
                                                  cx.ident_f[0:4, 0:4].unsqueeze(1).to_broadcast([4, NB, 4]), ALU.mult),
                 reads=["dec", "consts"], writes=["decx"])
            S.op("pe", lambda pe: pe.matmul(cx.ps[0][0:96, :NB * 4], cx.ones1[0:4, 0:96],
                                            decx[:].rearrange("p b h -> p (b h)"), start=True, stop=True),
                 reads=["decx", "consts"], writes=["ps0"])
            S.op("dve", lambda e: e.tensor_copy(decb[:], cx.ps[0][0:96, :NB * 4]), reads=["ps0"], writes=["decb"])
            for blk in range(NB):
                ps = cx.ps[1]

                def mm(pe, blk=blk, ps=ps):
                    pe.matmul(ps[:, blk * 8:blk * 8 + 4], g_al[:, blk * 128:(blk + 1) * 128], cx.ident_f[0:4, 0:4],
                              start=True, stop=True)
                    return pe.matmul(ps[:, blk * 8 + 4:blk * 8 + 8], g_fl[:, blk * 128:(blk + 1) * 128],
                                     cx.ident_f[0:4, 0:4], start=True, stop=True)
                S.op("pe", mm, reads=["g_al", "g_fl", "consts"], writes=["ps1"])
            S.op("dve", lambda e: e.tensor_copy(tokmaj[:].rearrange("p b c -> p (b c)"), cx.ps[1][:, :NB * 8]),
                 reads=["ps1"], writes=["tokmaj"])
            S.op("dve", lambda e: e.tensor_copy(Mc[:, 0:1], g_M[:, T - 1:T]), reads=["g_M", "Mprev"], writes=["Mc"])
            S.op("dve", lambda e: e.tensor_copy(Fc[:, 0:1], g_F[:, T - 1:T]), reads=["g_F"], writes=["Fc"])
            for blk in range(NB):
                tb = slice(blk * 128, (blk + 1) * 128)
                pS = cx.ps[4]
                for h in range(NH):
                    for half in range(2):
                        hh = h * 2 + half
                        S.op("pool", lambda e, hh=hh, blk=blk, h=h: e.tensor_scalar(
                            Cbf[:, hh, :], Cst[:, hh, :], decb[:, blk * 4 + h:blk * 4 + h + 1], None, ALU.mult),
                            reads=[("Cst", hh), "decb"], writes=[("Cbf", hh)])

                    def mm_s(pe, h=h, tb=tb):
                        for half in range(2):
                            i = pe.matmul(pS[:, h * 128:(h + 1) * 128], qkT[:, 8 + h * 2 + half, tb], qkT[:, h * 2 + half, tb],
                                          start=(half == 0), stop=(half == 1))
                        return i
                    S.op("pe", mm_s, reads=[("qkT", 8 + h * 2), ("qkT", 9 + h * 2), ("qkT", h * 2), ("qkT", h * 2 + 1)],
                         writes=[("pS", h)])
                    S.op("dve", lambda e, h=h, blk=blk: e.scalar_tensor_tensor(
                        out=scT[:, h, :], in0=pS[:, h * 128:(h + 1) * 128], scalar=tokmaj[:, blk, h:h + 1],
                        in1=cx.cmask[:], op0=ALU.mult, op1=ALU.mult),
                        reads=[("pS", h), "tokmaj", "consts"], writes=[("scT", h)])

                    def mm_t(pe, h=h, tb=tb):
                        for half in range(2):
                            i = pe.transpose(psb[:, h * DH + half * 96:h * DH + (half + 1) * 96],
                                             qkT[:, 8 + h * 2 + half, tb], cx.ident_b[0:96, 0:96])
                        return i
                    S.op("pe", mm_t, reads=[("qkT", 8 + h * 2), ("qkT", 9 + h * 2), "consts"], writes=[("psb", h)])
                    S.op("act", lambda e, h=h, blk=blk: e.activation(
                        kpr[:, h, :], psb[:, h * DH:(h + 1) * DH], AF.Copy, scale=tokmaj[:, blk, h:h + 1]),
                        reads=[("psb", h), "tokmaj"], writes=[("kpr", h)])
                    pX = cx.ps[2 + h // 2]
                    xo = (h % 2) * 256

                    def mm_x(pe, h=h, tb=tb, blk=blk, pX=pX, xo=xo):
                        pe.matmul(pX[:, xo:xo + DH + 1], scT[:, h, :], vaug[:, blk, h, :], start=True, stop=False)
                        pe.matmul(pX[:, xo:xo + DH + 1], qkT[:, h * 2, tb], Cbf[:, h * 2, :], start=False, stop=False)
                        return pe.matmul(pX[:, xo:xo + DH + 1], qkT[:, h * 2 + 1, tb], Cbf[:, h * 2 + 1, :],
                                         start=False, stop=True)
                    S.op("pe", mm_x, reads=[("scT", h), "vaug", ("qkT", h * 2), ("qkT", h * 2 + 1),
                                            ("Cbf", h * 2), ("Cbf", h * 2 + 1)], writes=[("pX", h)])
                    pU = cx.ps[6 + h % 2]

                    def mm_u(pe, h=h, blk=blk, pU=pU):
                        for half in range(2):
                            i = pe.matmul(pU[0:96, half * 256:half * 256 + DH + 1], kpr[:, h, half * 96:(half + 1) * 96],
                                          vaug[:, blk, h, :], start=True, stop=True)
                        return i
                    S.op("pe", mm_u, reads=[("kpr", h), "vaug"], writes=[f"ps{6 + h % 2}"])
                    for half in range(2):
                        hh = h * 2 + half
                        S.op("dve", lambda e, hh=hh, half=half, blk=blk, h=h, pU=pU: e.scalar_tensor_tensor(
                            out=Cst[:, hh, :], in0=Cst[:, hh, :], scalar=decb[:, blk * 4 + h:blk * 4 + h + 1],
                            in1=pU[0:96, half * 256:half * 256 + DH + 1], op0=ALU.mult, op1=ALU.add),
                            reads=[("Cst", hh), "decb", f"ps{6 + h % 2}"], writes=[("Cst", hh)])
                    S.op("dve", lambda e, h=h, blk=blk, pX=pX, xo=xo: e.tensor_scalar(
                        dn[:, h:h + 1], pX[:, xo + DH:xo + DH + 1], -1.0, tokmaj[:, blk, 4 + h:5 + h], ALU.mult, ALU.max),
                        reads=[("pX", h), "tokmaj"], writes=[("dn", h)])
                    S.op("dve", lambda e, h=h, pX=pX, xo=xo: e.tensor_tensor(
                        dn[:, h:h + 1], dn[:, h:h + 1], pX[:, xo + DH:xo + DH + 1], ALU.max),
                        reads=[("pX", h), ("dn", h)], writes=[("dn", h)])
                    S.op("dve", lambda e, h=h: e.reciprocal(dn[:, h:h + 1], dn[:, h:h + 1]),
                         reads=[("dn", h)], writes=[("dn", h)])
                    S.op("act", lambda e, h=h, pX=pX, xo=xo: e.activation(hx[:, h, :], pX[:, xo:xo + DH], AF.Copy,
                                                                         scale=dn[:, h:h + 1]),
                         reads=[("pX", h), ("dn", h)], writes=[("hx", h)])
                    S.op("dve", lambda e, h=h: e.bn_stats(bst[:, h, :], hx[:, h, :]), reads=[("hx", h)], writes=[("bst", h)])
                    S.op("dve", lambda e, h=h: e.bn_aggr(mv[:, h, :], bst[:, h, :]), reads=[("bst", h)], writes=[("mv", h)])
                mvk = [("mv", h) for h in range(NH)]
                S.op("dve", lambda e: e.tensor_scalar(rs4[:].unsqueeze(2), mv[:, :, 1:2], EPS, None, ALU.add),
                     reads=mvk, writes=["rs4"])
                S.op("pool", lambda e: e.tensor_tensor(rs4[:], rs4[:], cx.mhalf[:, 0:NH], ALU.pow),
                     reads=["rs4", "consts"], writes=["rs4"])
                for h in range(NH):
                    S.op("dve", lambda e, h=h: e.tensor_scalar(hx[:, h, :], hx[:, h, :], mv[:, h, 0:1], rs4[:, h:h + 1],
                                                               ALU.subtract, ALU.mult),
                         reads=[("hx", h), ("mv", h), "rs4"], writes=[("hx", h)])
                S.op("pool", lambda e: e.tensor_tensor(hn[:], hx[:].rearrange("p h e -> p (h e)"), cx.hng[:], ALU.mult),
                     reads=[("hx", h) for h in range(NH)] + ["consts"], writes=["hn"])

                def mm_tr(pe):
                    for j in range(6):
                        i = pe.transpose(psb[:, j * 128:(j + 1) * 128], hn[:, j * 128:(j + 1) * 128], cx.ident_b[:])
                    return i
                S.op("pe", mm_tr, reads=["hn", "consts"], writes=[("psb", h) for h in range(NH)])
                S.op("dve", lambda e, tb=tb: e.tensor_tensor(
                    mixin[:, 0:6, tb], psb[:, 0:768].rearrange("p (j t) -> p j t", t=128), sigo[:, :, tb], ALU.mult),
                    reads=[("psb", h) for h in range(NH)] + [("sigo", j) for j in range(6)],
                    writes=[("mixin", j) for j in range(6)])
            xattn_out_ln(S, cx, X, xk, xb, win, 4 * DTOK + 8, wout, mixin, qT, ebuf, rden, kv, T, gcol, bcol)
            S.dma("sp", f"xst{sl}", out=dst_v[:, :, ti * T:(ti + 1) * T], in_=X[:],
                  reads=[xk], writes=["dst"])
        S.barrier()


def build_program(NT, stages=("f00",), debug=False, T_FFN=256, T_MIX=256):
    nc = bass.Bass("TRN2", target_bir_lowering=False)
    cx = Ctx()
    cx.nc = nc

    def din(name, shape):
        return nc.dram_tensor(name, shape, F32, kind="ExternalInput").ap()
    xT = din("xT", [D, NT])
    memT = din("memT", [D, NMEM])
    smalls_d = din("smalls", [128, 256])
    ident_d = din("ident", [128, 128])
    cmask_d = din("cmask", [128, 128])
    hng_d = din("hng", [128, NH * DH])
    w = {}
    w["ffn_w_gate"] = din("ffn_w_gate", [2, 2, D, DFF])
    w["ffn_w_up"] = din("ffn_w_up", [2, 2, D, DFF])
    w["ffn_w_down"] = din("ffn_w_down", [2, 2, DFF, D])
    w["w_kv_mem"] = din("w_kv_mem", [2, D, 2 * DXA])
    w["w_out"] = din("w_out", [2, D, D])
    w["w_in_conv"] = din("w_in_conv", [1, D, D_IN_CONV])
    w["w_in_mlstm"] = din("w_in_mlstm", [1, D, D_IN_ML])
    outT = nc.dram_tensor("outT", [D, NT], F32, kind="ExternalOutput").ap()
    nst = len(stages)
    bufs = []
    if debug:
        for i in range(nst - 1):
            bufs.append(nc.dram_tensor(f"act{i}", [D, NT], F32, kind="ExternalOutput").ap())
    else:
        pp = [nc.dram_tensor(f"scr{i}", [D, NT], F32, kind="Internal").ap() for i in range(2)]
        for i in range(nst - 1):
            bufs.append(pp[i % 2])
    bufs.append(outT)

    with ExitStack() as es:
        S = Sched(nc, es)
        cx.ps = [es.enter_context(nc.psum_tensor(f"ps{i}", [128, 512], F32)) for i in range(8)]
        cx.smalls = es.enter_context(nc.sbuf_tensor("smalls_sb", [128, 256], F32))
        cx.ones_f = es.enter_context(nc.sbuf_tensor("ones_f", [128, 128], F32))
        cx.mhalf = es.enter_context(nc.sbuf_tensor("mhalf", [128, 512], F32))
        cx.ln_zsq = es.enter_context(nc.sbuf_tensor("ln_zsq", [128, KC, 512], F32))
        cx.ln_mean = es.enter_context(nc.sbuf_tensor("ln_mean", [128, 512], F32))
        cx.ln_var = es.enter_context(nc.sbuf_tensor("ln_var", [128, 512], F32))
        cx.ones1 = es.enter_context(nc.sbuf_tensor("ones1", [128, 128], F32))
        cx.ident_f = es.enter_context(nc.sbuf_tensor("ident_f", [128, 128], F32))
        cx.ident_b = es.enter_context(nc.sbuf_tensor("ident_b", [128, 128], BF16))
        cx.cmask = es.enter_context(nc.sbuf_tensor("cmask_sb", [128, 128], F32))
        cx.hng = es.enter_context(nc.sbuf_tensor("hng_sb", [128, NH * DH], F32))
        S.dma("sp", "consts", out=cx.smalls[:], in_=smalls_d, writes=["consts"])
        S.dma("sp", "consts", out=cx.ident_f[:], in_=ident_d, writes=["consts"])
        S.dma("pool", "consts2", out=cx.ident_b[:], in_=ident_d, writes=["consts_b"])
        S.dma("sp", "consts", out=cx.cmask[:], in_=cmask_d, writes=["consts"])
        S.dma("sp", "consts", out=cx.hng[:], in_=hng_d, writes=["consts"])
        S.op("dve", lambda e: e.memset(cx.ones1[:], 1.0), writes=["consts"])
        S.op("dve", lambda e: e.memset(cx.ones_f[:], 1.0 / D), writes=["consts"])
        S.op("dve", lambda e: e.memset(cx.mhalf[:], -0.5), writes=["consts"])
        S.barrier()
        cur = xT
        for si, st in enumerate(stages):
            dst = bufs[si]
            kind, l, i = st[0], int(st[1]), int(st[2]) if len(st) > 2 else 0
            if kind == "f":
                col = (l * 3 + (0 if i == 0 else 2)) * 16
                ffn_stage(S, cx, cur, dst, w["ffn_w_gate"][l, i], w["ffn_w_up"][l, i], w["ffn_w_down"][l, i],
                          col, col + 8, NT, T_FFN, si)
            elif kind == "m" and l == 0:
                col = (l * 3 + 1) * 16
                mix0_stage(S, cx, cur, dst, w["w_in_conv"][0], w["w_out"][0], w["w_kv_mem"][0], memT,
                           col, col + 8, NT, T_MIX, si)
            elif kind == "m" and l == 1:
                col = (l * 3 + 1) * 16
                mix1_stage(S, cx, cur, dst, w["w_in_mlstm"][0], w["w_out"][1], w["w_kv_mem"][1], memT,
                           col, col + 8, NT, T_MIX, si)
            cur = dst
        S.barrier()
    return nc


def make_smalls(inputs):
    sm = np.zeros((128, 256), np.float32)
    ln_g, ln_b = np.asarray(inputs["ln_g"]), np.asarray(inputs["ln_b"])
    for l in range(2):
        for i in range(3):
            c = (l * 3 + i) * 16
            sm[:, c:c + 8] = ln_g[l, i].reshape(8, 128).T
            sm[:, c + 8:c + 16] = ln_b[l, i].reshape(8, 128).T
    cw = np.asarray(inputs["conv_w"])[0]
    for tap in range(3):
        sm[:, 96 + tap * 6:96 + tap * 6 + 6] = cw[tap].reshape(6, 128).T
    qw = np.asarray(inputs["qk_conv_w"])[0]
    for tap in range(4):
        for g in range(16):
            qk, h, half = g // 8, (g % 8) // 2, g % 2
            c0 = qk * DTOK + h * DH + half * 96
            sm[0:96, QKW + tap * 16 + g] = qw[tap, c0:c0 + 96]
    bg = np.asarray(inputs["b_gates"])[0]
    sm[0:4, GB] = bg[0:4]
    sm[0:4, GB + 1] = bg[4:8]
    return sm


WNAMES = ["ffn_w_gate", "ffn_w_up", "ffn_w_down", "w_kv_mem", "w_out", "w_in_conv", "w_in_mlstm"]


def make_inmap(inputs, b, NT, shared=None):
    if shared is None:
        shared = {k: np.ascontiguousarray(np.asarray(inputs[k], dtype=np.float32)) for k in WNAMES}
        shared["smalls"] = make_smalls(inputs)
        shared["ident"] = np.eye(128, dtype=np.float32)
        shared["cmask"] = np.triu(np.ones((128, 128), np.float32))
        shared["hng"] = np.ascontiguousarray(np.broadcast_to(
            np.asarray(inputs["head_norm_g"], np.float32).reshape(1, NH * DH), (128, NH * DH)))
    m = dict(shared)
    m["xT"] = np.ascontiguousarray(np.asarray(inputs["x"])[b, :NT].T)
    m["memT"] = np.ascontiguousarray(np.asarray(inputs["mem"])[b].T)
    return m


ALL_STAGES = ("f00", "m0", "f01", "f10", "m1", "f11")


def kernel(**inputs):
    x = np.asarray(inputs["x"])
    B, NT, _ = x.shape
    nc = build_program(NT, stages=ALL_STAGES)
    shared = None
    in_maps = []
    for b in range(B):
        m = make_inmap(inputs, b, NT, shared)
        if shared is None:
            shared = {k: v for k, v in m.items() if k not in ("xT", "memT")}
        in_maps.append(m)
    res = run_bass_kernel_spmd(nc, in_maps, core_ids=list(range(B)))
    out = np.stack([np.ascontiguousarray(np.asarray(r["outT"]).T) for r in res.results])
    return out.astype(np.float32)
```

```python
from contextlib import ExitStack
import numpy as np
import concourse.bass as bass
import concourse.mybir as mybir
from concourse.bass_utils import run_bass_kernel_spmd

F32 = mybir.dt.float32
BF16 = mybir.dt.bfloat16
AF = mybir.ActivationFunctionType
ALU = mybir.AluOpType
AX = mybir.AxisListType

D = 1024
DFF = 2816
NMEM = 256
DTOK = 768
DXA = 256
DH = 192
NH = 4
D_IN_CONV = 2560
D_IN_ML = 3336
EPS = 1e-5
ALPHA = 4.0 ** 0.25
KC = 8
FC = 22

SAME_ENG_SYNC = True


class Sched:
    def __init__(self, nc, es):
        self.nc = nc
        self.es = es
        self.h = {"pe": nc.tensor, "dve": nc.vector, "act": nc.scalar, "pool": nc.gpsimd, "sp": nc.sync}
        self.sem = {k: es.enter_context(nc.semaphore("s_" + k)) for k in self.h}
        self.cnt = {k: 0 for k in self.h}
        self.obs = {k: {} for k in self.h}
        self.lw = {}
        self.rd = {}
        self.dsem = {}

    def semh(self, sk):
        return self.sem[sk] if sk in self.sem else self.dsem[sk][0]

    def _deps(self, eng, reads, writes):
        need = {}

        def add(rec):
            if rec is None:
                return
            sk, v = rec
            if sk in self.dsem:
                v = self.dsem[sk][1]
            if need.get(sk, 0) < v:
                need[sk] = v

        for k in reads:
            add(self.lw.get(k))
        for k in writes:
            add(self.lw.get(k))
            for sk, v in self.rd.get(k, {}).items():
                add((sk, v))
        for sk, v in need.items():
            if self.obs[eng].get(sk, 0) >= v:
                continue
            if sk == eng and (eng == "pe" or eng == "sp" or not SAME_ENG_SYNC):
                continue
            self.obs[eng][sk] = v
            self.h[eng].wait_ge(self.semh(sk), v)

    def _record(self, rec, reads, writes):
        for k in reads:
            d = self.rd.setdefault(k, {})
            if d.get(rec[0], 0) < rec[1]:
                d[rec[0]] = rec[1]
        for k in writes:
            self.lw[k] = rec
            self.rd[k] = {}

    def op(self, eng, fn, reads=(), writes=()):
        self._deps(eng, reads, writes)
        ins = fn(self.h[eng])
        self.cnt[eng] += 1
        ins.then_inc(self.sem[eng], 1)
        self._record((eng, self.cnt[eng]), reads, writes)

    def dma(self, eng, semname, out, in_, reads=(), writes=()):
        if semname not in self.dsem:
            self.dsem[semname] = [self.es.enter_context(self.nc.semaphore("d_" + semname)), 0]
        self._deps(eng, reads, writes)
        ins = self.h[eng].dma_start(out=out, in_=in_)
        d = self.dsem[semname]
        d[1] += 16
        ins.then_inc(d[0], 16)
        self._record((semname, d[1]), reads, writes)

    def barrier(self):
        for e in self.h:
            for o in self.h:
                if o != e and self.cnt[o] > self.obs[e].get(o, 0):
                    self.obs[e][o] = self.cnt[o]
                    self.h[e].wait_ge(self.sem[o], self.cnt[o])
            for name, (hd, c) in self.dsem.items():
                if c > self.obs[e].get(name, 0):
                    self.obs[e][name] = c
                    self.h[e].wait_ge(hd, c)
        self.lw = {}
        self.rd = {}


class Ctx:
    pass


def fm(ap):
    return ap.rearrange("(c p) t -> p c t", p=128)


def layer_norm_inplace(S, cx, X, T, gcol, bcol, key, zsq=None, zkeys=("ln_zsq",)):
    nc = cx.nc
    ps_m, ps_q = cx.ps[6], cx.ps[7]
    mean, var = cx.ln_mean, cx.ln_var
    if zsq is None:
        zsq = cx.ln_zsq
    zkeys = list(zkeys)
    S.op("pool", lambda e: e.tensor_tensor(zsq[:, :, :T], X[:, :, :T], X[:, :, :T], ALU.mult),
         reads=[key], writes=zkeys)

    def mm_mean(pe):
        for c in range(KC):
            i = pe.matmul(ps_m[:, :T], cx.ones_f[:], X[:, c, :T], start=(c == 0), stop=(c == KC - 1))
        return i
    S.op("pe", mm_mean, reads=[key, "consts"], writes=["ps6"])

    def mm_sq(pe):
        for c in range(KC):
            i = pe.matmul(ps_q[:, :T], cx.ones_f[:], zsq[:, c, :T], start=(c == 0), stop=(c == KC - 1))
        return i
    S.op("pe", mm_sq, reads=zkeys + ["consts"], writes=["ps7"])
    S.op("dve", lambda e: e.tensor_copy(mean[:, :T], ps_m[:, :T]), reads=["ps6"], writes=["ln_mean"])
    S.op("dve", lambda e: e.tensor_tensor(var[:, :T], mean[:, :T], mean[:, :T], ALU.mult),
         reads=["ln_mean"], writes=["ln_var"])
    S.op("dve", lambda e: e.tensor_tensor(var[:, :T], ps_q[:, :T], var[:, :T], ALU.subtract),
         reads=["ps7", "ln_var"], writes=["ln_var"])
    S.op("dve", lambda e: e.tensor_scalar(var[:, :T], var[:, :T], EPS, None, ALU.add),
         reads=["ln_var"], writes=["ln_var"])
    S.op("pool", lambda e: e.tensor_tensor(var[:, :T], var[:, :T], cx.mhalf[:, :T], ALU.pow),
         reads=["ln_var", "consts"], writes=["ln_var"])
    mean_b = mean[:, :T].unsqueeze(1).to_broadcast([128, KC, T])
    rstd_b = var[:, :T].unsqueeze(1).to_broadcast([128, KC, T])
    g_b = cx.smalls[:, gcol:gcol + KC].unsqueeze(2).to_broadcast([128, KC, T])
    b_b = cx.smalls[:, bcol:bcol + KC].unsqueeze(2).to_broadcast([128, KC, T])
    S.op("dve", lambda e: e.tensor_tensor(X[:, :, :T], X[:, :, :T], mean_b, ALU.subtract),
         reads=[key, "ln_mean"], writes=[key])
    S.op("pool", lambda e: e.tensor_tensor(X[:, :, :T], X[:, :, :T], rstd_b, ALU.mult),
         reads=[key, "ln_var"], writes=[key])
    S.op("dve", lambda e: e.tensor_tensor(X[:, :, :T], X[:, :, :T], g_b, ALU.mult),
         reads=[key, "consts"], writes=[key])
    S.op("pool", lambda e: e.tensor_tensor(X[:, :, :T], X[:, :, :T], b_b, ALU.add),
         reads=[key, "consts"], writes=[key])


def ffn_stage(S, cx, src, dst, Wg, Wu, Wd, gcol, bcol, NT, T, sid):
    nc = cx.nc
    with ExitStack() as es:
        def sb(name, shape, dt):
            return es.enter_context(nc.sbuf_tensor(f"{name}_{sid}", shape, dt))
        wg = sb("wg", [128, KC, DFF], BF16)
        wu = sb("wu", [128, KC, DFF], BF16)
        wd = sb("wd", [128, FC, D], BF16)
        NXS = cx.ffn_nxs
        xs = [sb(f"xs{i}", [128, KC, T], F32) for i in range(NXS)]
        xb = sb("xb", [128, KC, T], BF16)
        hb = sb("hb", [128, FC, T], BF16)
        sg = [sb(f"sg{i}", [128, T], F32) for i in range(2)]
        zsq = hb[:].rearrange("p f t -> p (f t)").bitcast(F32)[:, 0:KC * T].rearrange("p (c t) -> p c t", c=KC)
        hbkeys = [("hb", f) for f in range(FC)]
        HF = DFF // 2
        Wg_v = Wg.rearrange("(kc p) f -> p kc f", p=128)
        Wu_v = Wu.rearrange("(kc p) f -> p kc f", p=128)
        Wd_v = Wd.rearrange("(fc p) d -> p fc d", p=128)
        for hf in range(2):
            S.dma("pool", f"wg{hf}", out=wg[:, :, hf * HF:(hf + 1) * HF], in_=Wg_v[:, :, hf * HF:(hf + 1) * HF],
                  writes=[("wg", hf)])
            S.dma("pool", f"wu{hf}", out=wu[:, :, hf * HF:(hf + 1) * HF], in_=Wu_v[:, :, hf * HF:(hf + 1) * HF],
                  writes=[("wu", hf)])
        for q in range(2):
            S.dma("pool", f"wd{q}", out=wd[:, q * 11:(q + 1) * 11, :], in_=Wd_v[:, q * 11:(q + 1) * 11, :],
                  writes=[("wd", q)])
        src_v, dst_v = fm(src), fm(dst)
        ntile = NT // T

        def load(ti):
            sl = ti % NXS
            S.dma("sp", f"xld{sl}", out=xs[sl][:], in_=src_v[:, :, ti * T:(ti + 1) * T],
                  reads=["src"], writes=[("xs", sl)])
        load(0)
        for ti in range(ntile):
            sl = ti % NXS
            X = xs[sl]
            xk = ("xs", sl)
            if NXS == 2 and ti + 1 < ntile:
                load(ti + 1)
            S.op("act", lambda e: e.activation(xb[:], X[:], AF.Copy), reads=[xk], writes=["xb"])
            S.op("pool", lambda e: e.tensor_scalar(X[:], X[:], ALPHA, None, ALU.mult), reads=[xk], writes=[xk])
            for fc in range(FC):
                hf = fc // 11
                pg, pu = cx.ps[fc % 2], cx.ps[2 + fc % 2]

                def mm_g(pe, fc=fc, pg=pg):
                    for c in range(KC):
                        i = pe.matmul(pg[:, :T], wg[:, c, fc * 128:(fc + 1) * 128], xb[:, c, :],
                                      start=(c == 0), stop=(c == KC - 1))
                    return i

                def mm_u(pe, fc=fc, pu=pu):
                    for c in range(KC):
                        i = pe.matmul(pu[:, :T], wu[:, c, fc * 128:(fc + 1) * 128], xb[:, c, :],
                                      start=(c == 0), stop=(c == KC - 1))
                    return i
                S.op("pe", mm_g, reads=[("wg", hf), "xb"], writes=[f"ps{fc % 2}"])
                S.op("pe", mm_u, reads=[("wu", hf), "xb"], writes=[f"ps{2 + fc % 2}"])
                sgt = sg[fc % 2]
                S.op("act", lambda e, pg=pg, sgt=sgt: e.activation(sgt[:], pg[:, :T], AF.Silu),
                     reads=[f"ps{fc % 2}"], writes=[("sg", fc % 2)])
                S.op("dve", lambda e, pu=pu, sgt=sgt, fc=fc: e.tensor_tensor(hb[:, fc, :], sgt[:], pu[:, :T], ALU.mult),
                     reads=[("sg", fc % 2), f"ps{2 + fc % 2}"], writes=[("hb", fc)])
            for dc in range(KC):
                py = cx.ps[4 + dc % 2]

                def mm_d(pe, dc=dc, py=py):
                    for f in range(FC):
                        i = pe.matmul(py[:, :T], wd[:, f, dc * 128:(dc + 1) * 128], hb[:, f, :],
                                      start=(f == 0), stop=(f == FC - 1))
                    return i
                S.op("pe", mm_d, reads=[("wd", 0), ("wd", 1)] + [("hb", f) for f in range(FC)],
                     writes=[f"ps{4 + dc % 2}"])
                S.op("dve", lambda e, dc=dc, py=py: e.scalar_tensor_tensor(
                    out=X[:, dc, :], in0=py[:, :T], scalar=0.5, in1=X[:, dc, :], op0=ALU.mult, op1=ALU.add),
                    reads=[f"ps{4 + dc % 2}", xk], writes=[xk])
            layer_norm_inplace(S, cx, X, T, gcol, bcol, xk, zsq=zsq, zkeys=hbkeys)
            S.dma("sp", f"xst{sl}", out=dst_v[:, :, ti * T:(ti + 1) * T], in_=X[:],
                  reads=[xk], writes=["dst"])
            if NXS == 1 and ti + 1 < ntile:
                load(ti + 1)
        S.barrier()


def load_cast(S, sem, out, in_, key, nsplit=1, axis_len=None):
    S.dma("pool", sem, out=out, in_=in_, writes=[key])


def kv_precompute(S, cx, sb, memT, Wkv):
    nc = cx.nc
    memb = sb("memb", [128, KC, NMEM], BF16)
    wkv = sb("wkv", [128, KC, 2 * DXA], BF16)
    kT = sb("kT", [64, NH, NMEM], BF16)
    vpad = sb("vpad", [128, 2, NH, 128], BF16)
    onespad = sb("onespad", [128, 2, 128], BF16)
    S.dma("pool", "memb", out=memb[:], in_=memT.rearrange("(c p) m -> p c m", p=128), writes=["memb"])
    S.dma("pool", "wkv", out=wkv[:], in_=Wkv.rearrange("(c p) n -> p c n", p=128), writes=["wkv"])
    S.op("dve", lambda e: e.memset(vpad[:], 0.0), writes=["vpad"])
    S.op("dve", lambda e: e.memset(onespad[:], 0.0), writes=["onespad"])
    S.op("dve", lambda e: e.memset(onespad[:, 0, 0:64], 1.0), writes=["onespad"])
    S.op("dve", lambda e: e.memset(onespad[:, 1, 64:128], 1.0), writes=["onespad"])
    for h in range(NH):
        ps = cx.ps[h % 2]

        def mm(pe, h=h, ps=ps):
            for c in range(KC):
                i = pe.matmul(ps[0:64, :NMEM], wkv[:, c, 64 * h:64 * h + 64], memb[:, c, :],
                              start=(c == 0), stop=(c == KC - 1))
            return i
        S.op("pe", mm, reads=["memb", "wkv"], writes=[f"ps{h % 2}"])
        S.op("dve", lambda e, h=h, ps=ps: e.tensor_copy(kT[:, h, :], ps[0:64, :NMEM]),
             reads=[f"ps{h % 2}"], writes=["kT"])
    for mc in range(2):
        ps = cx.ps[2 + mc]

        def mm(pe, mc=mc, ps=ps):
            for c in range(KC):
                i = pe.matmul(ps[:, :DXA], memb[:, c, mc * 128:(mc + 1) * 128], wkv[:, c, DXA:2 * DXA],
                              start=(c == 0), stop=(c == KC - 1))
            return i
        S.op("pe", mm, reads=["memb", "wkv"], writes=[f"ps{2 + mc}"])
        for h in range(NH):
            o = 64 * (h % 2)
            S.op("dve", lambda e, h=h, mc=mc, ps=ps, o=o: e.tensor_copy(vpad[:, mc, h, o:o + 64], ps[:, 64 * h:64 * h + 64]),
                 reads=[f"ps{2 + mc}"], writes=["vpad"])
    return kT, vpad, onespad


def xattn_out_ln(S, cx, X, xk, xb, win, qcol, wout, mixin, qT, ebuf, rden, kv, T, gcol, bcol):
    kT, vpad, onespad = kv
    for h in range(NH):
        ps = cx.ps[h % 2]

        def mm(pe, h=h, ps=ps):
            for c in range(KC):
                i = pe.matmul(ps[0:64, :T], win[:, c, qcol + 64 * h:qcol + 64 * h + 64], xb[:, c, :T],
                              start=(c == 0), stop=(c == KC - 1))
            return i
        S.op("pe", mm, reads=["win", "xb"], writes=[f"ps{h % 2}"])
        S.op("dve", lambda e, h=h, ps=ps: e.tensor_copy(qT[:, h, :T], ps[0:64, :T]),
             reads=[f"ps{h % 2}"], writes=[("qT", h)])
    for h in range(NH):
        for mc in range(2):
            i8 = h * 2 + mc
            ps = cx.ps[2 + i8 % 2]
            S.op("pe", lambda pe, h=h, mc=mc, ps=ps: pe.matmul(ps[:, :T], kT[:, h, mc * 128:(mc + 1) * 128], qT[:, h, :T],
                                                             start=True, stop=True),
                 reads=["kT", ("qT", h)], writes=[f"ps{2 + i8 % 2}"])
            S.op("act", lambda e, i8=i8, ps=ps: e.activation(ebuf[:, i8, :T], ps[:, :T], AF.Exp, scale=0.125),
                 reads=[f"ps{2 + i8 % 2}"], writes=[("e", i8)])
    for pr in range(2):
        po, pd = cx.ps[4], cx.ps[5]

        def mm_o(pe, pr=pr):
            n = 0
            for h in (2 * pr, 2 * pr + 1):
                for mc in range(2):
                    i = pe.matmul(po[:, :T], vpad[:, mc, h, :], ebuf[:, h * 2 + mc, :T], start=(n == 0), stop=(n == 3))
                    n += 1
            return i

        def mm_d(pe, pr=pr):
            n = 0
            for h in (2 * pr, 2 * pr + 1):
                for mc in range(2):
                    i = pe.matmul(pd[:, :T], onespad[:, h % 2, :], ebuf[:, h * 2 + mc, :T], start=(n == 0), stop=(n == 3))
                    n += 1
            return i
        er = [("e", i) for i in range(4 * pr, 4 * pr + 4)]
        S.op("pe", mm_o, reads=["vpad"] + er, writes=["ps4"])
        S.op("pe", mm_d, reads=["onespad"] + er, writes=["ps5"])
        S.op("dve", lambda e: e.reciprocal(rden[:, :T], pd[:, :T]), reads=["ps5"], writes=["rden"])
        S.op("dve", lambda e, pr=pr: e.tensor_tensor(mixin[:, 6 + pr, :T], po[:, :T], rden[:, :T], ALU.mult),
             reads=["ps4", "rden"], writes=[("mixin", 6 + pr)])
    for dc in range(KC):
        py = cx.ps[dc % 2]

        def mm_y(pe, dc=dc, py=py):
            for f in range(KC):
                i = pe.matmul(py[:, :T], wout[:, f, dc * 128:(dc + 1) * 128], mixin[:, f, :T],
                              start=(f == 0), stop=(f == KC - 1))
            return i
        S.op("pe", mm_y, reads=["wout"] + [("mixin", f) for f in range(KC)], writes=[f"ps{dc % 2}"])
        S.op("dve", lambda e, dc=dc, py=py: e.scalar_tensor_tensor(
            out=X[:, dc, :T], in0=X[:, dc, :T], scalar=ALPHA, in1=py[:, :T], op0=ALU.mult, op1=ALU.add),
            reads=[f"ps{dc % 2}", xk], writes=[xk])
    layer_norm_inplace(S, cx, X, T, gcol, bcol, xk)


def mix0_stage(S, cx, src, dst, Win, Wout, Wkv, memT, gcol, bcol, NT, T, sid):
    nc = cx.nc
    with ExitStack() as es:
        def sb(name, shape, dt):
            return es.enter_context(nc.sbuf_tensor(f"{name}_{sid}", shape, dt))
        win = sb("win", [128, KC, D_IN_CONV], BF16)
        cx.ln_zsq = sb("zsq", [128, KC, T], F32)
        wout = sb("wout", [128, KC, D], BF16)
        xs = [sb(f"xs{i}", [128, KC, T], F32) for i in range(2)]
        xb = sb("xb", [128, KC, T], BF16)
        mixin = sb("mixin", [128, KC, T], BF16)
        cxb = sb("cxb", [128, 6, T + 2], F32)
        csb = [sb(f"csb{i}", [128, T], F32) for i in range(2)]
        acc = [sb(f"acc{i}", [128, T], F32) for i in range(2)]
        qT = sb("qT", [64, NH, T], BF16)
        ebuf = sb("ebuf", [128, 8, T], BF16)
        rden = sb("rden", [128, T], F32)
        Win_v = Win.rearrange("(kc p) f -> p kc f", p=128)
        for q in range(4):
            S.dma("pool", "win", out=win[:, :, q * 640:(q + 1) * 640], in_=Win_v[:, :, q * 640:(q + 1) * 640],
                  writes=["win"])
        S.dma("pool", "wout", out=wout[:], in_=Wout.rearrange("(kc p) f -> p kc f", p=128), writes=["wout"])
        kv = kv_precompute(S, cx, sb, memT, Wkv)
        S.op("dve", lambda e: e.memset(cxb[:], 0.0), writes=[("cxb", j) for j in range(6)])
        src_v, dst_v = fm(src), fm(dst)
        ntile = NT // T
        CW = 96

        def load(ti):
            sl = ti % 2
            S.dma("sp", f"xld{sl}", out=xs[sl][:], in_=src_v[:, :, ti * T:(ti + 1) * T],
                  reads=["src"], writes=[("xs", sl)])
        load(0)
        for ti in range(ntile):
            sl = ti % 2
            X = xs[sl]
            xk = ("xs", sl)
            if ti + 1 < ntile:
                load(ti + 1)
            S.op("act", lambda e: e.activation(xb[:], X[:], AF.Copy), reads=[xk], writes=["xb"])
            for j in range(6):
                b3 = 3 * (j % 2)
                pc, px, pb = cx.ps[b3], cx.ps[b3 + 1], cx.ps[b3 + 2]

                def mm(pe, ps, col):
                    for c in range(KC):
                        i = pe.matmul(ps[:, :T], win[:, c, col:col + 128], xb[:, c, :],
                                      start=(c == 0), stop=(c == KC - 1))
                    return i
                S.op("pe", lambda pe, pc=pc, j=j: mm(pe, pc, DTOK + j * 128), reads=["win", "xb"], writes=[f"ps{b3}"])
                S.op("pe", lambda pe, px=px, j=j: mm(pe, px, 2 * DTOK + j * 128), reads=["win", "xb"], writes=[f"ps{b3 + 1}"])
                S.op("pe", lambda pe, pb=pb, j=j: mm(pe, pb, j * 128), reads=["win", "xb"], writes=[f"ps{b3 + 2}"])
                cs, ac = csb[j % 2], acc[j % 2]
                S.op("act", lambda e, cs=cs, pc=pc: e.activation(cs[:], pc[:, :T], AF.Copy),
                     reads=[f"ps{b3}"], writes=[("csb", j % 2)])
                S.op("dve", lambda e, j=j, cs=cs, px=px: e.tensor_tensor(cxb[:, j, 2:2 + T], cs[:], px[:, :T], ALU.mult),
                     reads=[("csb", j % 2), f"ps{b3 + 1}"], writes=[("cxb", j)])
                w0 = cx.smalls[:, CW + 0 * 6 + j:CW + 0 * 6 + j + 1]
                w1 = cx.smalls[:, CW + 1 * 6 + j:CW + 1 * 6 + j + 1]
                w2 = cx.smalls[:, CW + 2 * 6 + j:CW + 2 * 6 + j + 1]
                S.op("pool", lambda e, j=j, ac=ac, w2=w2: e.tensor_scalar(ac[:], cxb[:, j, 2:2 + T], w2, None, ALU.mult),
                     reads=[("cxb", j), "consts"], writes=[("acc", j % 2)])
                S.op("dve", lambda e, j=j, ac=ac, w1=w1: e.scalar_tensor_tensor(
                    out=ac[:], in0=cxb[:, j, 1:1 + T], scalar=w1, in1=ac[:], op0=ALU.mult, op1=ALU.add),
                    reads=[("cxb", j), ("acc", j % 2)], writes=[("acc", j % 2)])
                S.op("dve", lambda e, j=j, ac=ac, w0=w0: e.scalar_tensor_tensor(
                    out=ac[:], in0=cxb[:, j, 0:T], scalar=w0, in1=ac[:], op0=ALU.mult, op1=ALU.add),
                    reads=[("cxb", j), ("acc", j % 2)], writes=[("acc", j % 2)])
                S.op("dve", lambda e, j=j, ac=ac, pb=pb: e.tensor_tensor(mixin[:, j, :], ac[:], pb[:, :T], ALU.mult),
                     reads=[("acc", j % 2), f"ps{b3 + 2}"], writes=[("mixin", j)])
                S.op("pool", lambda e, j=j: e.tensor_copy(cxb[:, j, 0:2], cxb[:, j, T:T + 2]),
                     reads=[("cxb", j)], writes=[("cxb", j)])
            xattn_out_ln(S, cx, X, xk, xb, win, 3 * DTOK, wout, mixin, qT, ebuf, rden, kv, T, gcol, bcol)
            S.dma("sp", f"xst{sl}", out=dst_v[:, :, ti * T:(ti + 1) * T], in_=X[:],
                  reads=[xk], writes=["dst"])
        S.barrier()


LOG_QSCALE = float(np.log(DH ** -0.5))
QKW = 120
GB = 184


def mix1_stage(S, cx, src, dst, Win, Wout, Wkv, memT, gcol, bcol, NT, T, sid):
    nc = cx.nc
    NB = T // 128
    with ExitStack() as es:
        def sb(name, shape, dt):
            return es.enter_context(nc.sbuf_tensor(f"{name}_{sid}", shape, dt))
        win = sb("win", [128, KC, D_IN_ML], BF16)
        cx.ln_zsq = sb("zsq", [128, KC, T], F32)
        cx.ones1 = sb("ones1", [128, 128], F32)
        cx.ident_f = sb("ident_f", [128, 128], F32)
        cx.ident_b = sb("ident_b", [128, 128], BF16)
        cx.cmask = sb("cmask_sb", [128, 128], F32)
        cx.hng = sb("hng_sb", [128, NH * DH], F32)
        ident_d, cmask_d, hng_d = cx.cdram
        S.dma("sp", "consts", out=cx.ident_f[:], in_=ident_d, writes=["consts"])
        S.dma("pool", "consts2", out=cx.ident_b[:], in_=ident_d, writes=["consts"])
        S.dma("sp", "consts", out=cx.cmask[:], in_=cmask_d, writes=["consts"])
        S.dma("sp", "consts", out=cx.hng[:], in_=hng_d, writes=["consts"])
        S.op("dve", lambda e: e.memset(cx.ones1[:], 1.0), writes=["consts"])
        S.barrier()
        wout = sb("wout", [128, KC, D], BF16)
        xs = [sb(f"xs{i}", [128, KC, T], F32) for i in range(2)]
        xb = sb("xb", [128, KC, T], BF16)
        mixin = sb("mixin", [128, KC, T], BF16)
        qT = sb("qT", [64, NH, T], BF16)
        ebuf = sb("ebuf", [128, 8, T], BF16)
        rden = sb("rden", [128, T], F32)
        raw = sb("raw", [96, 2, T + 3], F32)
        carry = sb("carry", [96, 16, 3], F32)
        cacc = sb("cacc", [96, 2, T], F32)
        qkT = sb("qkT", [96, 16, T], BF16)
        vaug = sb("vaug", [128, NB, NH, DH + 1], BF16)
        sigo = sb("sigo", [128, 6, T], F32)
        g_i = sb("g_i", [4, T], F32)
        g_x = sb("g_x", [4, T], F32)
        g_t = sb("g_t", [4, T], F32)
        g_f = sb("g_f", [4, T], F32)
        g_F = sb("g_F", [4, T], F32)
        g_a = sb("g_a", [4, T], F32)
        g_M = sb("g_M", [4, T], F32)
        g_al = sb("g_al", [4, T], F32)
        g_fl = sb("g_fl", [4, T], F32)
        g_one = sb("g_one", [4, T], F32)
        Fc = sb("Fc", [4, 1], F32)
        Mc = sb("Mc", [4, 1], F32)
        Mprev = sb("Mprev", [4, NB], F32)
        dec = sb("dec", [4, NB], F32)
        decx = sb("decx", [4, NB, 4], F32)
        tokmaj = sb("tokmaj", [128, NB, 8], F32)
        decb = sb("decb", [96, NB * 4], F32)
        Cst = sb("Cst", [96, 8, DH + 1], F32)
        Cbf = sb("Cbf", [96, 8, DH + 1], BF16)
        scT = sb("scT", [128, NH, 128], BF16)
        kpr = sb("kpr", [128, NH, DH], BF16)
        hx = sb("hx", [128, NH, DH], F32)
        hn = sb("hn", [128, NH * DH], BF16)
        dn = sb("dn", [128, NH], F32)
        bst = sb("bst", [128, NH, 6], F32)
        mv = sb("mv", [128, NH, 2], F32)
        rs4 = sb("rs4", [128, NH], F32)

        Win_v = Win.rearrange("(kc p) f -> p kc f", p=128)
        for q in range(4):
            S.dma("pool", "win", out=win[:, :, q * 834:(q + 1) * 834], in_=Win_v[:, :, q * 834:(q + 1) * 834],
                  writes=["win"])
        S.dma("pool", "wout", out=wout[:], in_=Wout.rearrange("(kc p) f -> p kc f", p=128), writes=["wout"])
        kv = kv_precompute(S, cx, sb, memT, Wkv)
        S.op("dve", lambda e: e.memset(carry[:], 0.0), writes=["carry"])
        S.op("dve", lambda e: e.memset(vaug[:], 1.0), writes=["vaug"])
        S.op("dve", lambda e: e.memset(Cst[:], 0.0), writes=["Cst"])
        S.op("dve", lambda e: e.memset(Fc[:], 0.0), writes=["Fc"])
        S.op("dve", lambda e: e.memset(Mc[:], 0.0), writes=["Mc"])
        S.op("dve", lambda e: e.memset(g_one[:], 1.0), writes=["g_one"])
        psb = cx.ps[5][:].bitcast(BF16)
        src_v, dst_v = fm(src), fm(dst)
        ntile = NT // T

        def load(ti):
            sl = ti % 2
            S.dma("sp", f"xld{sl}", out=xs[sl][:], in_=src_v[:, :, ti * T:(ti + 1) * T],
                  reads=["src"], writes=[("xs", sl)])
        load(0)
        for ti in range(ntile):
            sl = ti % 2
            X = xs[sl]
            xk = ("xs", sl)
            if ti + 1 < ntile:
                load(ti + 1)
            S.op("act", lambda e: e.activation(xb[:], X[:], AF.Copy), reads=[xk], writes=["xb"])

            def proj(ps, pkey, col, m, rows=None):
                def mm(pe):
                    for c in range(KC):
                        i = pe.matmul(ps[0:m, :T], win[:, c, col:col + m], xb[:, c, :],
                                      start=(c == 0), stop=(c == KC - 1))
                    return i
                S.op("pe", mm, reads=["win", "xb"], writes=[pkey])
            for g in range(16):
                qk, h, half = g // 8, (g % 8) // 2, g % 2
                col = qk * DTOK + h * DH + half * 96
                r = g % 2
                ps = cx.ps[r]
                proj(ps, f"ps{r}", col, 96)
                S.op("act", lambda e, r=r, ps=ps: e.activation(raw[:, r, 3:3 + T], ps[0:96, :T], AF.Copy),
                     reads=[f"ps{r}"], writes=[("raw", r)])
                S.op("pool", lambda e, r=r, g=g: e.tensor_copy(raw[:, r, 0:3], carry[:, g, :]),
                     reads=["carry"], writes=[("raw", r)])
                wt = [cx.smalls[0:96, QKW + tap * 16 + g:QKW + tap * 16 + g + 1] for tap in range(4)]
                S.op("pool", lambda e, r=r, wt=wt: e.tensor_scalar(cacc[:, r, :], raw[:, r, 3:3 + T], wt[3], None, ALU.mult),
                     reads=[("raw", r), "consts"], writes=[("cacc", r)])
                for tap in (2, 1, 0):
                    S.op("dve", lambda e, r=r, wt=wt, tap=tap: e.scalar_tensor_tensor(
                        out=cacc[:, r, :], in0=raw[:, r, tap:tap + T], scalar=wt[tap], in1=cacc[:, r, :],
                        op0=ALU.mult, op1=ALU.add),
                        reads=[("raw", r), ("cacc", r), "consts"], writes=[("cacc", r)])
                S.op("pool", lambda e, r=r, g=g: e.tensor_copy(carry[:, g, :], raw[:, r, T:T + 3]),
                     reads=[("raw", r)], writes=["carry"])
                S.op("act", lambda e, r=r, g=g: e.activation(qkT[:, g, :], cacc[:, r, :], AF.Silu),
                     reads=[("cacc", r)], writes=[("qkT", g)])
            for blk in range(NB):
                for hp in range(2):
                    ps = cx.ps[hp]

                    def mm(pe, blk=blk, hp=hp, ps=ps):
                        for c in range(KC):
                            i = pe.matmul(ps[:, :2 * DH], xb[:, c, blk * 128:(blk + 1) * 128],
                                          win[:, c, 2 * DTOK + hp * 2 * DH:2 * DTOK + (hp + 1) * 2 * DH],
                                          start=(c == 0), stop=(c == KC - 1))
                        return i
                    S.op("pe", mm, reads=["win", "xb"], writes=[f"ps{hp}"])
                    S.op("dve", lambda e, blk=blk, hp=hp, ps=ps: e.tensor_copy(
                        vaug[:, blk, 2 * hp:2 * hp + 2, 0:DH], ps[:, :2 * DH].rearrange("p (h e) -> p h e", h=2)),
                        reads=[f"ps{hp}"], writes=["vaug"])
            for j in range(6):
                ps = cx.ps[j % 2]
                proj(ps, f"ps{j % 2}", 3 * DTOK + j * 128, 128)
                S.op("act", lambda e, j=j, ps=ps: e.activation(sigo[:, j, :], ps[:, :T], AF.Sigmoid),
                     reads=[f"ps{j % 2}"], writes=[("sigo", j)])
            proj(cx.ps[0], "ps0", 4 * DTOK, 4)
            proj(cx.ps[1], "ps1", 4 * DTOK + 4, 4)
            S.op("dve", lambda e: e.tensor_scalar(g_i[:], cx.ps[0][0:4, :T], cx.smalls[0:4, GB:GB + 1], None, ALU.add),
                 reads=["ps0", "consts"], writes=["g_i"])
            S.op("dve", lambda e: e.tensor_scalar(g_x[:], cx.ps[1][0:4, :T], cx.smalls[0:4, GB + 1:GB + 2], None, ALU.add),
                 reads=["ps1", "consts"], writes=["g_x"])
            S.op("dve", lambda e: e.tensor_scalar(g_t[:], g_x[:], -1.0, None, ALU.mult), reads=["g_x"], writes=["g_t"])
            S.op("dve", lambda e: e.tensor_tensor(g_t[:], g_t[:], g_x[:], ALU.max), reads=["g_x", "g_t"], writes=["g_t"])
            S.op("act", lambda e: e.activation(g_t[:], g_t[:], AF.Exp, scale=-1.0), reads=["g_t"], writes=["g_t"])
            S.op("act", lambda e: e.activation(g_t[:], g_t[:], AF.Ln, bias=1.0), reads=["g_t"], writes=["g_t"])
            S.op("dve", lambda e: e.tensor_scalar(g_f[:], g_x[:], 0.0, None, ALU.min), reads=["g_x"], writes=["g_f"])
            S.op("dve", lambda e: e.tensor_tensor(g_f[:], g_f[:], g_t[:], ALU.subtract), reads=["g_f", "g_t"], writes=["g_f"])
            S.op("dve", lambda e: e.tensor_tensor_scan(g_F[:], g_one[:], g_f[:], Fc[:, 0:1], ALU.mult, ALU.add),
                 reads=["g_f", "g_one", "Fc"], writes=["g_F"])
            S.op("dve", lambda e: e.tensor_tensor(g_a[:], g_i[:], g_F[:], ALU.subtract), reads=["g_i", "g_F"], writes=["g_a"])
            S.op("dve", lambda e: e.tensor_tensor_scan(g_M[:], g_a[:], g_a[:], Mc[:, 0:1], ALU.max, ALU.max),
                 reads=["g_a", "Mc"], writes=["g_M"])
            Mend = g_M[:].rearrange("p (b t) -> p b t", t=128)[:, :, 127:128]
            Mend_b = Mend.to_broadcast([4, NB, 128])
            S.op("dve", lambda e: e.tensor_copy(Mprev[:, 0:1], Mc[:, 0:1]), reads=["Mc"], writes=["Mprev"])
            if NB > 1:
                S.op("dve", lambda e: e.tensor_copy(Mprev[:, 1:NB].unsqueeze(2), Mend[:, 0:NB - 1, :]),
                     reads=["g_M"], writes=["Mprev"])
            S.op("dve", lambda e: e.tensor_tensor(g_al[:].rearrange("p (b t) -> p b t", t=128),
                                                  g_a[:].rearrange("p (b t) -> p b t", t=128), Mend_b, ALU.subtract),
                 reads=["g_a", "g_M"], writes=["g_al"])
            S.op("dve", lambda e: e.tensor_scalar(g_al[:], g_al[:], LOG_QSCALE, None, ALU.add), reads=["g_al"], writes=["g_al"])
            S.op("act", lambda e: e.activation(g_al[:], g_al[:], AF.Exp), reads=["g_al"], writes=["g_al"])
            S.op("dve", lambda e: e.tensor_tensor(g_fl[:].rearrange("p (b t) -> p b t", t=128),
                                                  g_F[:].rearrange("p (b t) -> p b t", t=128), Mend_b, ALU.add),
                 reads=["g_F", "g_M"], writes=["g_fl"])
            S.op("act", lambda e: e.activation(g_fl[:], g_fl[:], AF.Exp, scale=-1.0), reads=["g_fl"], writes=["g_fl"])
            S.op("dve", lambda e: e.tensor_tensor(dec[:].unsqueeze(2), Mprev[:].unsqueeze(2), Mend, ALU.subtract),
                 reads=["Mprev", "g_M"], writes=["dec"])
            S.op("act", lambda e: e.activation(dec[:], dec[:], AF.Exp), reads=["dec"], writes=["dec"])
            S.op("dve", lambda e: e.tensor_tensor(decx[:], dec[:].unsqueeze(2).to_broadcast([4, NB, 4]),
                                                  cx.ident_f[0:4, 0:4].unsqueeze(1).to_broadcast([4, NB, 4]), ALU.mult),
                 reads=["dec", "consts"], writes=["decx"])
            S.op("pe", lambda pe: pe.matmul(cx.ps[0][0:96, :NB * 4], cx.ones1[0:4, 0:96],
                                            decx[:].rearrange("p b h -> p (b h)"), start=True, stop=True),
                 reads=["decx", "consts"], writes=["ps0"])
            S.op("dve", lambda e: e.tensor_copy(decb[:], cx.ps[0][0:96, :NB * 4]), reads=["ps0"], writes=["decb"])
            for blk in range(NB):
                ps = cx.ps[1]

                def mm(pe, blk=blk, ps=ps):
                    pe.matmul(ps[:, blk * 8:blk * 8 + 4], g_al[:, blk * 128:(blk + 1) * 128], cx.ident_f[0:4, 0:4],
                              start=True, stop=True)
                    return pe.matmul(ps[:, blk * 8 + 4:blk * 8 + 8], g_fl[:, blk * 128:(blk + 1) * 128],
                                     cx.ident_f[0:4, 0:4], start=True, stop=True)
                S.op("pe", mm, reads=["g_al", "g_fl", "consts"], writes=["ps1"])
            S.op("dve", lambda e: e.tensor_copy(tokmaj[:].rearrange("p b c -> p (b c)"), cx.ps[1][:, :NB * 8]),
                 reads=["ps1"], writes=["tokmaj"])
            S.op("dve", lambda e: e.tensor_copy(Mc[:, 0:1], g_M[:, T - 1:T]), reads=["g_M", "Mprev"], writes=["Mc"])
            S.op("dve", lambda e: e.tensor_copy(Fc[:, 0:1], g_F[:, T - 1:T]), reads=["g_F"], writes=["Fc"])
            for blk in range(NB):
                tb = slice(blk * 128, (blk + 1) * 128)
                pS = cx.ps[4]
                for h in range(NH):
                    for half in range(2):
                        hh = h * 2 + half
                        S.op("pool", lambda e, hh=hh, blk=blk, h=h: e.tensor_scalar(
                            Cbf[:, hh, :], Cst[:, hh, :], decb[:, blk * 4 + h:blk * 4 + h + 1], None, ALU.mult),
                            reads=[("Cst", hh), "decb"], writes=[("Cbf", hh)])

                    def mm_s(pe, h=h, tb=tb):
                        for half in range(2):
                            i = pe.matmul(pS[:, h * 128:(h + 1) * 128], qkT[:, 8 + h * 2 + half, tb], qkT[:, h * 2 + half, tb],
                                          start=(half == 0), stop=(half == 1))
                        return i
                    S.op("pe", mm_s, reads=[("qkT", 8 + h * 2), ("qkT", 9 + h * 2), ("qkT", h * 2), ("qkT", h * 2 + 1)],
                         writes=[("pS", h)])
                    S.op("dve", lambda e, h=h, blk=blk: e.scalar_tensor_tensor(
                        out=scT[:, h, :], in0=pS[:, h * 128:(h + 1) * 128], scalar=tokmaj[:, blk, h:h + 1],
                        in1=cx.cmask[:], op0=ALU.mult, op1=ALU.mult),
                        reads=[("pS", h), "tokmaj", "consts"], writes=[("scT", h)])

                    def mm_t(pe, h=h, tb=tb):
                        for half in range(2):
                            i = pe.transpose(psb[:, h * DH + half * 96:h * DH + (half + 1) * 96],
                                             qkT[:, 8 + h * 2 + half, tb], cx.ident_b[0:96, 0:96])
                        return i
                    S.op("pe", mm_t, reads=[("qkT", 8 + h * 2), ("qkT", 9 + h * 2), "consts"], writes=[("psb", h)])
                    S.op("act", lambda e, h=h, blk=blk: e.activation(
                        kpr[:, h, :], psb[:, h * DH:(h + 1) * DH], AF.Copy, scale=tokmaj[:, blk, h:h + 1]),
                        reads=[("psb", h), "tokmaj"], writes=[("kpr", h)])
                    pX = cx.ps[2 + h // 2]
                    xo = (h % 2) * 256

                    def mm_x(pe, h=h, tb=tb, blk=blk, pX=pX, xo=xo):
                        pe.matmul(pX[:, xo:xo + DH + 1], scT[:, h, :], vaug[:, blk, h, :], start=True, stop=False)
                        pe.matmul(pX[:, xo:xo + DH + 1], qkT[:, h * 2, tb], Cbf[:, h * 2, :], start=False, stop=False)
                        return pe.matmul(pX[:, xo:xo + DH + 1], qkT[:, h * 2 + 1, tb], Cbf[:, h * 2 + 1, :],
                                         start=False, stop=True)
                    S.op("pe", mm_x, reads=[("scT", h), "vaug", ("qkT", h * 2), ("qkT", h * 2 + 1),
                                            ("Cbf", h * 2), ("Cbf", h * 2 + 1)], writes=[("pX", h)])
                    pU = cx.ps[6 + h % 2]

                    def mm_u(pe, h=h, blk=blk, pU=pU):
                        for half in range(2):
                            i = pe.matmul(pU[0:96, half * 256:half * 256 + DH + 1], kpr[:, h, half * 96:(half + 1) * 96],
                                          vaug[:, blk, h, :], start=True, stop=True)
                        return i
                    S.op("pe", mm_u, reads=[("kpr", h), "vaug"], writes=[f"ps{6 + h % 2}"])
                    for half in range(2):
                        hh = h * 2 + half
                        S.op("dve", lambda e, hh=hh, half=half, blk=blk, h=h, pU=pU: e.scalar_tensor_tensor(
                            out=Cst[:, hh, :], in0=Cst[:, hh, :], scalar=decb[:, blk * 4 + h:blk * 4 + h + 1],
                            in1=pU[0:96, half * 256:half * 256 + DH + 1], op0=ALU.mult, op1=ALU.add),
                            reads=[("Cst", hh), "decb", f"ps{6 + h % 2}"], writes=[("Cst", hh)])
                    S.op("dve", lambda e, h=h, blk=blk, pX=pX, xo=xo: e.tensor_scalar(
                        dn[:, h:h + 1], pX[:, xo + DH:xo + DH + 1], -1.0, tokmaj[:, blk, 4 + h:5 + h], ALU.mult, ALU.max),
                        reads=[("pX", h), "tokmaj"], writes=[("dn", h)])
                    S.op("dve", lambda e, h=h, pX=pX, xo=xo: e.tensor_tensor(
                        dn[:, h:h + 1], dn[:, h:h + 1], pX[:, xo + DH:xo + DH + 1], ALU.max),
                        reads=[("pX", h), ("dn", h)], writes=[("dn", h)])
                    S.op("dve", lambda e, h=h: e.reciprocal(dn[:, h:h + 1], dn[:, h:h + 1]),
                         reads=[("dn", h)], writes=[("dn", h)])
                    S.op("act", lambda e, h=h, pX=pX, xo=xo: e.activation(hx[:, h, :], pX[:, xo:xo + DH], AF.Copy,
                                                                         scale=dn[:, h:h + 1]),
                         reads=[("pX", h), ("dn", h)], writes=[("hx", h)])
                    S.op("dve", lambda e, h=h: e.bn_stats(bst[:, h, :], hx[:, h, :]), reads=[("hx", h)], writes=[("bst", h)])
                    S.op("dve", lambda e, h=h: e.bn_aggr(mv[:, h, :], bst[:, h, :]), reads=[("bst", h)], writes=[("mv", h)])
                mvk = [("mv", h) for h in range(NH)]
                S.op("dve", lambda e: e.tensor_scalar(rs4[:].unsqueeze(2), mv[:, :, 1:2], EPS, None, ALU.add),
                     reads=mvk, writes=["rs4"])
                S.op("pool", lambda e: e.tensor_tensor(rs4[:], rs4[:], cx.mhalf[:, 0:NH], ALU.pow),
                     reads=["rs4", "consts"], writes=["rs4"])
                for h in range(NH):
                    S.op("dve", lambda e, h=h: e.tensor_scalar(hx[:, h, :], hx[:, h, :], mv[:, h, 0:1], rs4[:, h:h + 1],
                                                               ALU.subtract, ALU.mult),
                         reads=[("hx", h), ("mv", h), "rs4"], writes=[("hx", h)])
                S.op("pool", lambda e: e.tensor_tensor(hn[:], hx[:].rearrange("p h e -> p (h e)"), cx.hng[:], ALU.mult),
                     reads=[("hx", h) for h in range(NH)] + ["consts"], writes=["hn"])

                def mm_tr(pe):
                    for j in range(6):
                        i = pe.transpose(psb[:, j * 128:(j + 1) * 128], hn[:, j * 128:(j + 1) * 128], cx.ident_b[:])
                    return i
                S.op("pe", mm_tr, reads=["hn", "consts"], writes=[("psb", h) for h in range(NH)])
                S.op("dve", lambda e, tb=tb: e.tensor_tensor(
                    mixin[:, 0:6, tb], psb[:, 0:768].rearrange("p (j t) -> p j t", t=128), sigo[:, :, tb], ALU.mult),
                    reads=[("psb", h) for h in range(NH)] + [("sigo", j) for j in range(6)],
                    writes=[("mixin", j) for j in range(6)])
            xattn_out_ln(S, cx, X, xk, xb, win, 4 * DTOK + 8, wout, mixin, qT, ebuf, rden, kv, T, gcol, bcol)
            S.dma("sp", f"xst{sl}", out=dst_v[:, :, ti * T:(ti + 1) * T], in_=X[:],
                  reads=[xk], writes=["dst"])
        S.barrier()


def build_program(NT, stages=("f00",), debug=False, T_FFN=512, T_MIX=256, FFN_NXS=2):
    nc = bass.Bass("TRN2", target_bir_lowering=False)
    cx = Ctx()
    cx.nc = nc

    def din(name, shape):
        return nc.dram_tensor(name, shape, F32, kind="ExternalInput").ap()
    xT = din("xT", [D, NT])
    memT = din("memT", [D, NMEM])
    smalls_d = din("smalls", [128, 256])
    ident_d = din("ident", [128, 128])
    cmask_d = din("cmask", [128, 128])
    hng_d = din("hng", [128, NH * DH])
    w = {}
    w["ffn_w_gate"] = din("ffn_w_gate", [2, 2, D, DFF])
    w["ffn_w_up"] = din("ffn_w_up", [2, 2, D, DFF])
    w["ffn_w_down"] = din("ffn_w_down", [2, 2, DFF, D])
    w["w_kv_mem"] = din("w_kv_mem", [2, D, 2 * DXA])
    w["w_out"] = din("w_out", [2, D, D])
    w["w_in_conv"] = din("w_in_conv", [1, D, D_IN_CONV])
    w["w_in_mlstm"] = din("w_in_mlstm", [1, D, D_IN_ML])
    outT = nc.dram_tensor("outT", [D, NT], F32, kind="ExternalOutput").ap()
    nst = len(stages)
    bufs = []
    if debug:
        for i in range(nst - 1):
            bufs.append(nc.dram_tensor(f"act{i}", [D, NT], F32, kind="ExternalOutput").ap())
    else:
        pp = [nc.dram_tensor(f"scr{i}", [D, NT], F32, kind="Internal").ap() for i in range(2)]
        for i in range(nst - 1):
            bufs.append(pp[i % 2])
    bufs.append(outT)

    with ExitStack() as es:
        S = Sched(nc, es)
        cx.ps = [es.enter_context(nc.psum_tensor(f"ps{i}", [128, 512], F32)) for i in range(8)]
        cx.smalls = es.enter_context(nc.sbuf_tensor("smalls_sb", [128, 256], F32))
        cx.ones_f = es.enter_context(nc.sbuf_tensor("ones_f", [128, 128], F32))
        cx.mhalf = es.enter_context(nc.sbuf_tensor("mhalf", [128, 512], F32))
        cx.ln_zsq = None
        cx.ffn_nxs = FFN_NXS
        cx.ln_mean = es.enter_context(nc.sbuf_tensor("ln_mean", [128, 512], F32))
        cx.ln_var = es.enter_context(nc.sbuf_tensor("ln_var", [128, 512], F32))
        cx.cdram = (ident_d, cmask_d, hng_d)
        S.dma("sp", "consts", out=cx.smalls[:], in_=smalls_d, writes=["consts"])
        S.op("dve", lambda e: e.memset(cx.ones_f[:], 1.0 / D), writes=["consts"])
        S.op("dve", lambda e: e.memset(cx.mhalf[:], -0.5), writes=["consts"])
        S.barrier()
        cur = xT
        for si, st in enumerate(stages):
            dst = bufs[si]
            kind, l, i = st[0], int(st[1]), int(st[2]) if len(st) > 2 else 0
            if kind == "f":
                col = (l * 3 + (0 if i == 0 else 2)) * 16
                ffn_stage(S, cx, cur, dst, w["ffn_w_gate"][l, i], w["ffn_w_up"][l, i], w["ffn_w_down"][l, i],
                          col, col + 8, NT, T_FFN, si)
            elif kind == "m" and l == 0:
                col = (l * 3 + 1) * 16
                mix0_stage(S, cx, cur, dst, w["w_in_conv"][0], w["w_out"][0], w["w_kv_mem"][0], memT,
                           col, col + 8, NT, T_MIX, si)
            elif kind == "m" and l == 1:
                col = (l * 3 + 1) * 16
                mix1_stage(S, cx, cur, dst, w["w_in_mlstm"][0], w["w_out"][1], w["w_kv_mem"][1], memT,
                           col, col + 8, NT, T_MIX, si)
            cur = dst
        S.barrier()
    return nc


def make_smalls(inputs):
    sm = np.zeros((128, 256), np.float32)
    ln_g, ln_b = np.asarray(inputs["ln_g"]), np.asarray(inputs["ln_b"])
    for l in range(2):
        for i in range(3):
            c = (l * 3 + i) * 16
            sm[:, c:c + 8] = ln_g[l, i].reshape(8, 128).T
            sm[:, c + 8:c + 16] = ln_b[l, i].reshape(8, 128).T
    cw = np.asarray(inputs["conv_w"])[0]
    for tap in range(3):
        sm[:, 96 + tap * 6:96 + tap * 6 + 6] = cw[tap].reshape(6, 128).T
    qw = np.asarray(inputs["qk_conv_w"])[0]
    for tap in range(4):
        for g in range(16):
            qk, h, half = g // 8, (g % 8) // 2, g % 2
            c0 = qk * DTOK + h * DH + half * 96
            sm[0:96, QKW + tap * 16 + g] = qw[tap, c0:c0 + 96]
    bg = np.asarray(inputs["b_gates"])[0]
    sm[0:4, GB] = bg[0:4]
    sm[0:4, GB + 1] = bg[4:8]
    return sm


WNAMES = ["ffn_w_gate", "ffn_w_up", "ffn_w_down", "w_kv_mem", "w_out", "w_in_conv", "w_in_mlstm"]


def make_inmap(inputs, b, NT, shared=None):
    if shared is None:
        shared = {k: np.ascontiguousarray(np.asarray(inputs[k], dtype=np.float32)) for k in WNAMES}
        shared["smalls"] = make_smalls(inputs)
        shared["ident"] = np.eye(128, dtype=np.float32)
        shared["cmask"] = np.triu(np.ones((128, 128), np.float32))
        shared["hng"] = np.ascontiguousarray(np.broadcast_to(
            np.asarray(inputs["head_norm_g"], np.float32).reshape(1, NH * DH), (128, NH * DH)))
    m = dict(shared)
    m["xT"] = np.ascontiguousarray(np.asarray(inputs["x"])[b, :NT].T)
    m["memT"] = np.ascontiguousarray(np.asarray(inputs["mem"])[b].T)
    return m


ALL_STAGES = ("f00", "m0", "f01", "f10", "m1", "f11")


def kernel(**inputs):
    x = np.asarray(inputs["x"])
    B, NT, _ = x.shape
    nc = build_program(NT, stages=ALL_STAGES)
    shared = None
    in_maps = []
    for b in range(B):
        m = make_inmap(inputs, b, NT, shared)
        if shared is None:
            shared = {k: v for k, v in m.items() if k not in ("xT", "memT")}
        in_maps.append(m)
    res = run_bass_kernel_spmd(nc, in_maps, core_ids=list(range(B)))
    out = np.stack([np.ascontiguousarray(np.asarray(r["outT"]).T) for r in res.results])
    return out.astype(np.float32)
```

```python
from contextlib import ExitStack
import numpy as np
import concourse.bass as bass
import concourse.mybir as mybir
from concourse.bass_utils import run_bass_kernel_spmd

F32 = mybir.dt.float32
BF16 = mybir.dt.bfloat16
AF = mybir.ActivationFunctionType
ALU = mybir.AluOpType
AX = mybir.AxisListType

D = 1024
DFF = 2816
NMEM = 256
DTOK = 768
DXA = 256
DH = 192
NH = 4
D_IN_CONV = 2560
D_IN_ML = 3336
EPS = 1e-5
ALPHA = 4.0 ** 0.25
KC = 8
FC = 22

SAME_ENG_SYNC = True


class Sched:
    def __init__(self, nc, es):
        self.nc = nc
        self.es = es
        self.h = {"pe": nc.tensor, "dve": nc.vector, "act": nc.scalar, "pool": nc.gpsimd, "sp": nc.sync}
        self.sem = {k: es.enter_context(nc.semaphore("s_" + k)) for k in self.h}
        self.cnt = {k: 0 for k in self.h}
        self.obs = {k: {} for k in self.h}
        self.lw = {}
        self.rd = {}
        self.dsem = {}

    def semh(self, sk):
        return self.sem[sk] if sk in self.sem else self.dsem[sk][0]

    def _deps(self, eng, reads, writes):
        need = {}

        def add(rec):
            if rec is None:
                return
            sk, v = rec
            if sk in self.dsem:
                v = self.dsem[sk][1]
            if need.get(sk, 0) < v:
                need[sk] = v

        for k in reads:
            add(self.lw.get(k))
        for k in writes:
            add(self.lw.get(k))
            for sk, v in self.rd.get(k, {}).items():
                add((sk, v))
        for sk, v in need.items():
            if self.obs[eng].get(sk, 0) >= v:
                continue
            if sk == eng and (eng == "pe" or eng == "sp" or not SAME_ENG_SYNC):
                continue
            self.obs[eng][sk] = v
            self.h[eng].wait_ge(self.semh(sk), v)

    def _record(self, rec, reads, writes):
        for k in reads:
            d = self.rd.setdefault(k, {})
            if d.get(rec[0], 0) < rec[1]:
                d[rec[0]] = rec[1]
        for k in writes:
            self.lw[k] = rec
            self.rd[k] = {}

    def op(self, eng, fn, reads=(), writes=()):
        self._deps(eng, reads, writes)
        ins = fn(self.h[eng])
        self.cnt[eng] += 1
        ins.then_inc(self.sem[eng], 1)
        self._record((eng, self.cnt[eng]), reads, writes)

    def dma(self, eng, semname, out, in_, reads=(), writes=()):
        if semname not in self.dsem:
            self.dsem[semname] = [self.es.enter_context(self.nc.semaphore("d_" + semname)), 0]
        self._deps(eng, reads, writes)
        ins = self.h[eng].dma_start(out=out, in_=in_)
        d = self.dsem[semname]
        d[1] += 16
        ins.then_inc(d[0], 16)
        self._record((semname, d[1]), reads, writes)

    def barrier(self):
        for e in self.h:
            for o in self.h:
                if (o != e or e in ("pool", "act")) and self.cnt[o] > self.obs[e].get(o, 0):
                    self.obs[e][o] = self.cnt[o]
                    self.h[e].wait_ge(self.sem[o], self.cnt[o])
            for name, (hd, c) in self.dsem.items():
                if c > self.obs[e].get(name, 0):
                    self.obs[e][name] = c
                    self.h[e].wait_ge(hd, c)
        self.lw = {}
        self.rd = {}


class Ctx:
    pass


def fm(ap):
    return ap.rearrange("(c p) t -> p c t", p=128)


def layer_norm_inplace(S, cx, X, T, gcol, bcol, key, zsq=None, zkeys=("ln_zsq",)):
    nc = cx.nc
    ps_m, ps_q = cx.ps[6], cx.ps[7]
    mean, var = cx.ln_mean, cx.ln_var
    if zsq is None:
        zsq = cx.ln_zsq
    zkeys = list(zkeys)
    S.op("act", lambda e: e.activation(zsq[:, :, :T], X[:, :, :T], AF.Square),
         reads=[key], writes=zkeys)

    def mm_mean(pe):
        for c in range(KC):
            i = pe.matmul(ps_m[:, :T], cx.ones_f[:], X[:, c, :T], start=(c == 0), stop=(c == KC - 1))
        return i
    S.op("pe", mm_mean, reads=[key, "consts"], writes=["ps6"])

    def mm_sq(pe):
        for c in range(KC):
            i = pe.matmul(ps_q[:, :T], cx.ones_f[:], zsq[:, c, :T], start=(c == 0), stop=(c == KC - 1))
        return i
    S.op("pe", mm_sq, reads=zkeys + ["consts"], writes=["ps7"])
    S.op("dve", lambda e: e.tensor_copy(mean[:, :T], ps_m[:, :T]), reads=["ps6"], writes=["ln_mean"])
    S.op("dve", lambda e: e.tensor_tensor(var[:, :T], mean[:, :T], mean[:, :T], ALU.mult),
         reads=["ln_mean"], writes=["ln_var"])
    S.op("dve", lambda e: e.tensor_tensor(var[:, :T], ps_q[:, :T], var[:, :T], ALU.subtract),
         reads=["ps7", "ln_var"], writes=["ln_var"])
    S.op("dve", lambda e: e.tensor_scalar(var[:, :T], var[:, :T], EPS, None, ALU.add),
         reads=["ln_var"], writes=["ln_var"])
    S.op("act", lambda e: e.activation(var[:, :T], var[:, :T], AF.Sqrt), reads=["ln_var"], writes=["ln_var"])
    S.op("dve", lambda e: e.reciprocal(var[:, :T], var[:, :T]), reads=["ln_var"], writes=["ln_var"])
    mean_b = mean[:, :T].unsqueeze(1).to_broadcast([128, KC, T])
    rstd_b = var[:, :T].unsqueeze(1).to_broadcast([128, KC, T])
    g_b = cx.smalls[:, gcol:gcol + KC].unsqueeze(2).to_broadcast([128, KC, T])
    b_b = cx.smalls[:, bcol:bcol + KC].unsqueeze(2).to_broadcast([128, KC, T])
    S.op("dve", lambda e: e.tensor_tensor(X[:, :, :T], X[:, :, :T], mean_b, ALU.subtract),
         reads=[key, "ln_mean"], writes=[key])
    S.op("dve", lambda e: e.tensor_tensor(X[:, :, :T], X[:, :, :T], rstd_b, ALU.mult),
         reads=[key, "ln_var"], writes=[key])
    for c in range(KC):
        S.op("act", lambda e, c=c: e.activation(X[:, c, :T], X[:, c, :T], AF.Identity,
                                                bias=cx.smalls[:, bcol + c:bcol + c + 1],
                                                scale=cx.smalls[:, gcol + c:gcol + c + 1]),
             reads=[key, "consts"], writes=[key])


def ffn_stage(S, cx, src, dst, Wg, Wu, Wd, gcol, bcol, NT, T, sid):
    nc = cx.nc
    with ExitStack() as es:
        def sb(name, shape, dt):
            return es.enter_context(nc.sbuf_tensor(f"{name}_{sid}", shape, dt))
        wg = sb("wg", [128, KC, DFF], BF16)
        wu = sb("wu", [128, KC, DFF], BF16)
        wd = sb("wd", [128, FC, D], BF16)
        NXS = cx.ffn_nxs
        xs = [sb(f"xs{i}", [128, KC, T], F32) for i in range(NXS)]
        xb = sb("xb", [128, KC, T], BF16)
        hb = sb("hb", [128, FC, T], BF16)
        sg = [sb(f"sg{i}", [128, T], F32) for i in range(2)]
        zsq = hb[:].rearrange("p f t -> p (f t)").bitcast(F32)[:, 0:KC * T].rearrange("p (c t) -> p c t", c=KC)
        hbkeys = [("hb", f) for f in range(FC)]
        HF = DFF // 2
        Wg_v = Wg.rearrange("(kc p) f -> p kc f", p=128)
        Wu_v = Wu.rearrange("(kc p) f -> p kc f", p=128)
        Wd_v = Wd.rearrange("(fc p) d -> p fc d", p=128)
        for hf in range(2):
            S.dma("pool", f"wg{hf}", out=wg[:, :, hf * HF:(hf + 1) * HF], in_=Wg_v[:, :, hf * HF:(hf + 1) * HF],
                  writes=[("wg", hf)])
            S.dma("pool", f"wu{hf}", out=wu[:, :, hf * HF:(hf + 1) * HF], in_=Wu_v[:, :, hf * HF:(hf + 1) * HF],
                  writes=[("wu", hf)])
        for q in range(2):
            S.dma("pool", f"wd{q}", out=wd[:, q * 11:(q + 1) * 11, :], in_=Wd_v[:, q * 11:(q + 1) * 11, :],
                  writes=[("wd", q)])
        src_v, dst_v = fm(src), fm(dst)
        ntile = NT // T

        def load(ti):
            sl = ti % NXS
            S.dma("sp", f"xld{sl}", out=xs[sl][:], in_=src_v[:, :, ti * T:(ti + 1) * T],
                  reads=["src"], writes=[("xs", sl)])
        load(0)
        for ti in range(ntile):
            sl = ti % NXS
            X = xs[sl]
            xk = ("xs", sl)
            if NXS == 2 and ti + 1 < ntile:
                load(ti + 1)
            S.op("act", lambda e: e.activation(xb[:], X[:], AF.Copy), reads=[xk], writes=["xb"])
            S.op("act", lambda e: e.activation(X[:], X[:], AF.Identity, scale=ALPHA), reads=[xk], writes=[xk])
            for fc in range(FC):
                hf = fc // 11
                pg, pu = cx.ps[fc % 2], cx.ps[2 + fc % 2]

                def mm_g(pe, fc=fc, pg=pg):
                    for c in range(KC):
                        i = pe.matmul(pg[:, :T], wg[:, c, fc * 128:(fc + 1) * 128], xb[:, c, :],
                                      start=(c == 0), stop=(c == KC - 1))
                    return i

                def mm_u(pe, fc=fc, pu=pu):
                    for c in range(KC):
                        i = pe.matmul(pu[:, :T], wu[:, c, fc * 128:(fc + 1) * 128], xb[:, c, :],
                                      start=(c == 0), stop=(c == KC - 1))
                    return i
                S.op("pe", mm_g, reads=[("wg", hf), "xb"], writes=[f"ps{fc % 2}"])
                S.op("pe", mm_u, reads=[("wu", hf), "xb"], writes=[f"ps{2 + fc % 2}"])
                sgt = sg[fc % 2]
                S.op("act", lambda e, pg=pg, sgt=sgt: e.activation(sgt[:], pg[:, :T], AF.Silu),
                     reads=[f"ps{fc % 2}"], writes=[("sg", fc % 2)])
                S.op("dve", lambda e, pu=pu, sgt=sgt, fc=fc: e.tensor_tensor(hb[:, fc, :], sgt[:], pu[:, :T], ALU.mult),
                     reads=[("sg", fc % 2), f"ps{2 + fc % 2}"], writes=[("hb", fc)])
            for dc in range(KC):
                py = cx.ps[4 + dc % 2]

                def mm_d(pe, dc=dc, py=py):
                    for f in range(FC):
                        i = pe.matmul(py[:, :T], wd[:, f, dc * 128:(dc + 1) * 128], hb[:, f, :],
                                      start=(f == 0), stop=(f == FC - 1))
                    return i
                S.op("pe", mm_d, reads=[("wd", 0), ("wd", 1)] + [("hb", f) for f in range(FC)],
                     writes=[f"ps{4 + dc % 2}"])
                S.op("dve", lambda e, dc=dc, py=py: e.scalar_tensor_tensor(
                    out=X[:, dc, :], in0=py[:, :T], scalar=0.5, in1=X[:, dc, :], op0=ALU.mult, op1=ALU.add),
                    reads=[f"ps{4 + dc % 2}", xk], writes=[xk])
            layer_norm_inplace(S, cx, X, T, gcol, bcol, xk, zsq=zsq, zkeys=hbkeys)
            S.dma("sp", f"xst{sl}", out=dst_v[:, :, ti * T:(ti + 1) * T], in_=X[:],
                  reads=[xk], writes=["dst"])
            if NXS == 1 and ti + 1 < ntile:
                load(ti + 1)
        S.barrier()


def load_cast(S, sem, out, in_, key, nsplit=1, axis_len=None):
    S.dma("pool", sem, out=out, in_=in_, writes=[key])


def kv_precompute(S, cx, sb, memT, Wkv):
    nc = cx.nc
    memb = sb("memb", [128, KC, NMEM], BF16)
    wkv = sb("wkv", [128, KC, 2 * DXA], BF16)
    kT = sb("kT", [64, NH, NMEM], BF16)
    vpad = sb("vpad", [128, 2, NH, 128], BF16)
    onespad = sb("onespad", [128, 2, 128], BF16)
    S.dma("pool", "memb", out=memb[:], in_=memT.rearrange("(c p) m -> p c m", p=128), writes=["memb"])
    S.dma("pool", "wkv", out=wkv[:], in_=Wkv.rearrange("(c p) n -> p c n", p=128), writes=["wkv"])
    S.op("dve", lambda e: e.memset(vpad[:], 0.0), writes=["vpad"])
    S.op("dve", lambda e: e.memset(onespad[:], 0.0), writes=["onespad"])
    S.op("dve", lambda e: e.memset(onespad[:, 0, 0:64], 1.0), writes=["onespad"])
    S.op("dve", lambda e: e.memset(onespad[:, 1, 64:128], 1.0), writes=["onespad"])
    for h in range(NH):
        ps = cx.ps[h % 2]

        def mm(pe, h=h, ps=ps):
            for c in range(KC):
                i = pe.matmul(ps[0:64, :NMEM], wkv[:, c, 64 * h:64 * h + 64], memb[:, c, :],
                              start=(c == 0), stop=(c == KC - 1))
            return i
        S.op("pe", mm, reads=["memb", "wkv"], writes=[f"ps{h % 2}"])
        S.op("dve", lambda e, h=h, ps=ps: e.tensor_copy(kT[:, h, :], ps[0:64, :NMEM]),
             reads=[f"ps{h % 2}"], writes=["kT"])
    for mc in range(2):
        ps = cx.ps[2 + mc]

        def mm(pe, mc=mc, ps=ps):
            for c in range(KC):
                i = pe.matmul(ps[:, :DXA], memb[:, c, mc * 128:(mc + 1) * 128], wkv[:, c, DXA:2 * DXA],
                              start=(c == 0), stop=(c == KC - 1))
            return i
        S.op("pe", mm, reads=["memb", "wkv"], writes=[f"ps{2 + mc}"])
        for h in range(NH):
            o = 64 * (h % 2)
            S.op("dve", lambda e, h=h, mc=mc, ps=ps, o=o: e.tensor_copy(vpad[:, mc, h, o:o + 64], ps[:, 64 * h:64 * h + 64]),
                 reads=[f"ps{2 + mc}"], writes=["vpad"])
    return kT, vpad, onespad


def xattn_out_ln(S, cx, X, xk, xb, win, qcol, wout, mixin, qT, ebuf, rden, kv, T, gcol, bcol):
    kT, vpad, onespad = kv
    for h in range(NH):
        ps = cx.ps[h % 2]

        def mm(pe, h=h, ps=ps):
            for c in range(KC):
                i = pe.matmul(ps[0:64, :T], win[:, c, qcol + 64 * h:qcol + 64 * h + 64], xb[:, c, :T],
                              start=(c == 0), stop=(c == KC - 1))
            return i
        S.op("pe", mm, reads=["win", "xb"], writes=[f"ps{h % 2}"])
        S.op("dve", lambda e, h=h, ps=ps: e.tensor_copy(qT[:, h, :T], ps[0:64, :T]),
             reads=[f"ps{h % 2}"], writes=[("qT", h)])
    for h in range(NH):
        for mc in range(2):
            i8 = h * 2 + mc
            ps = cx.ps[2 + i8 % 2]
            S.op("pe", lambda pe, h=h, mc=mc, ps=ps: pe.matmul(ps[:, :T], kT[:, h, mc * 128:(mc + 1) * 128], qT[:, h, :T],
                                                             start=True, stop=True),
                 reads=["kT", ("qT", h)], writes=[f"ps{2 + i8 % 2}"])
            S.op("act", lambda e, i8=i8, ps=ps: e.activation(ebuf[:, i8, :T], ps[:, :T], AF.Exp, scale=0.125),
                 reads=[f"ps{2 + i8 % 2}"], writes=[("e", i8)])
    for pr in range(2):
        po, pd = cx.ps[4], cx.ps[5]

        def mm_o(pe, pr=pr):
            n = 0
            for h in (2 * pr, 2 * pr + 1):
                for mc in range(2):
                    i = pe.matmul(po[:, :T], vpad[:, mc, h, :], ebuf[:, h * 2 + mc, :T], start=(n == 0), stop=(n == 3))
                    n += 1
            return i

        def mm_d(pe, pr=pr):
            n = 0
            for h in (2 * pr, 2 * pr + 1):
                for mc in range(2):
                    i = pe.matmul(pd[:, :T], onespad[:, h % 2, :], ebuf[:, h * 2 + mc, :T], start=(n == 0), stop=(n == 3))
                    n += 1
            return i
        er = [("e", i) for i in range(4 * pr, 4 * pr + 4)]
        S.op("pe", mm_o, reads=["vpad"] + er, writes=["ps4"])
        S.op("pe", mm_d, reads=["onespad"] + er, writes=["ps5"])
        S.op("dve", lambda e: e.reciprocal(rden[:, :T], pd[:, :T]), reads=["ps5"], writes=["rden"])
        S.op("dve", lambda e, pr=pr: e.tensor_tensor(mixin[:, 6 + pr, :T], po[:, :T], rden[:, :T], ALU.mult),
             reads=["ps4", "rden"], writes=[("mixin", 6 + pr)])
    for dc in range(KC):
        py = cx.ps[dc % 2]

        def mm_y(pe, dc=dc, py=py):
            for f in range(KC):
                i = pe.matmul(py[:, :T], wout[:, f, dc * 128:(dc + 1) * 128], mixin[:, f, :T],
                              start=(f == 0), stop=(f == KC - 1))
            return i
        S.op("pe", mm_y, reads=["wout"] + [("mixin", f) for f in range(KC)], writes=[f"ps{dc % 2}"])
        S.op("dve", lambda e, dc=dc, py=py: e.scalar_tensor_tensor(
            out=X[:, dc, :T], in0=X[:, dc, :T], scalar=ALPHA, in1=py[:, :T], op0=ALU.mult, op1=ALU.add),
            reads=[f"ps{dc % 2}", xk], writes=[xk])
    layer_norm_inplace(S, cx, X, T, gcol, bcol, xk)


def mix0_stage(S, cx, src, dst, Win, Wout, Wkv, memT, gcol, bcol, NT, T, sid):
    nc = cx.nc
    with ExitStack() as es:
        def sb(name, shape, dt):
            return es.enter_context(nc.sbuf_tensor(f"{name}_{sid}", shape, dt))
        win = sb("win", [128, KC, D_IN_CONV], BF16)
        cx.ln_zsq = sb("zsq", [128, KC, T], F32)
        wout = sb("wout", [128, KC, D], BF16)
        xs = [sb(f"xs{i}", [128, KC, T], F32) for i in range(2)]
        xb = sb("xb", [128, KC, T], BF16)
        mixin = sb("mixin", [128, KC, T], BF16)
        cxb = sb("cxb", [128, 6, T + 2], F32)
        csb = [sb(f"csb{i}", [128, T], F32) for i in range(2)]
        acc = [sb(f"acc{i}", [128, T], F32) for i in range(2)]
        qT = sb("qT", [64, NH, T], BF16)
        ebuf = sb("ebuf", [128, 8, T], BF16)
        rden = sb("rden", [128, T], F32)
        Win_v = Win.rearrange("(kc p) f -> p kc f", p=128)
        for q in range(4):
            S.dma("pool", "win", out=win[:, :, q * 640:(q + 1) * 640], in_=Win_v[:, :, q * 640:(q + 1) * 640],
                  writes=["win"])
        S.dma("pool", "wout", out=wout[:], in_=Wout.rearrange("(kc p) f -> p kc f", p=128), writes=["wout"])
        kv = kv_precompute(S, cx, sb, memT, Wkv)
        S.op("dve", lambda e: e.memset(cxb[:], 0.0), writes=[("cxb", j) for j in range(6)])
        src_v, dst_v = fm(src), fm(dst)
        ntile = NT // T
        CW = 96

        def load(ti):
            sl = ti % 2
            S.dma("sp", f"xld{sl}", out=xs[sl][:], in_=src_v[:, :, ti * T:(ti + 1) * T],
                  reads=["src"], writes=[("xs", sl)])
        load(0)
        for ti in range(ntile):
            sl = ti % 2
            X = xs[sl]
            xk = ("xs", sl)
            if ti + 1 < ntile:
                load(ti + 1)
            S.op("act", lambda e: e.activation(xb[:], X[:], AF.Copy), reads=[xk], writes=["xb"])
            for j in range(6):
                b3 = 3 * (j % 2)
                pc, px, pb = cx.ps[b3], cx.ps[b3 + 1], cx.ps[b3 + 2]

                def mm(pe, ps, col):
                    for c in range(KC):
                        i = pe.matmul(ps[:, :T], win[:, c, col:col + 128], xb[:, c, :],
                                      start=(c == 0), stop=(c == KC - 1))
                    return i
                S.op("pe", lambda pe, pc=pc, j=j: mm(pe, pc, DTOK + j * 128), reads=["win", "xb"], writes=[f"ps{b3}"])
                S.op("pe", lambda pe, px=px, j=j: mm(pe, px, 2 * DTOK + j * 128), reads=["win", "xb"], writes=[f"ps{b3 + 1}"])
                S.op("pe", lambda pe, pb=pb, j=j: mm(pe, pb, j * 128), reads=["win", "xb"], writes=[f"ps{b3 + 2}"])
                cs, ac = csb[j % 2], acc[j % 2]
                S.op("act", lambda e, cs=cs, pc=pc: e.activation(cs[:], pc[:, :T], AF.Copy),
                     reads=[f"ps{b3}"], writes=[("csb", j % 2)])
                S.op("dve", lambda e, j=j, cs=cs, px=px: e.tensor_tensor(cxb[:, j, 2:2 + T], cs[:], px[:, :T], ALU.mult),
                     reads=[("csb", j % 2), f"ps{b3 + 1}"], writes=[("cxb", j)])
                w0 = cx.smalls[:, CW + 0 * 6 + j:CW + 0 * 6 + j + 1]
                w1 = cx.smalls[:, CW + 1 * 6 + j:CW + 1 * 6 + j + 1]
                w2 = cx.smalls[:, CW + 2 * 6 + j:CW + 2 * 6 + j + 1]
                S.op("dve", lambda e, j=j, ac=ac, w2=w2: e.tensor_scalar(ac[:], cxb[:, j, 2:2 + T], w2, None, ALU.mult),
                     reads=[("cxb", j), "consts"], writes=[("acc", j % 2)])
                S.op("dve", lambda e, j=j, ac=ac, w1=w1: e.scalar_tensor_tensor(
                    out=ac[:], in0=cxb[:, j, 1:1 + T], scalar=w1, in1=ac[:], op0=ALU.mult, op1=ALU.add),
                    reads=[("cxb", j), ("acc", j % 2)], writes=[("acc", j % 2)])
                S.op("dve", lambda e, j=j, ac=ac, w0=w0: e.scalar_tensor_tensor(
                    out=ac[:], in0=cxb[:, j, 0:T], scalar=w0, in1=ac[:], op0=ALU.mult, op1=ALU.add),
                    reads=[("cxb", j), ("acc", j % 2)], writes=[("acc", j % 2)])
                S.op("dve", lambda e, j=j, ac=ac, pb=pb: e.tensor_tensor(mixin[:, j, :], ac[:], pb[:, :T], ALU.mult),
                     reads=[("acc", j % 2), f"ps{b3 + 2}"], writes=[("mixin", j)])
                S.op("pool", lambda e, j=j: e.tensor_copy(cxb[:, j, 0:2], cxb[:, j, T:T + 2]),
                     reads=[("cxb", j)], writes=[("cxb", j)])
            xattn_out_ln(S, cx, X, xk, xb, win, 3 * DTOK, wout, mixin, qT, ebuf, rden, kv, T, gcol, bcol)
            S.dma("sp", f"xst{sl}", out=dst_v[:, :, ti * T:(ti + 1) * T], in_=X[:],
                  reads=[xk], writes=["dst"])
        S.barrier()


LOG_QSCALE = float(np.log(DH ** -0.5))
QKW = 120
GB = 184


def mix1_stage(S, cx, src, dst, Win, Wout, Wkv, memT, gcol, bcol, NT, T, sid):
    nc = cx.nc
    NB = T // 128
    with ExitStack() as es:
        def sb(name, shape, dt):
            return es.enter_context(nc.sbuf_tensor(f"{name}_{sid}", shape, dt))
        win = sb("win", [128, KC, D_IN_ML], BF16)
        cx.ln_zsq = sb("zsq", [128, KC, T], F32)
        cx.ones1 = sb("ones1", [128, 128], F32)
        cx.ident_f = sb("ident_f", [128, 128], F32)
        cx.ident_b = sb("ident_b", [128, 128], BF16)
        cx.cmask = sb("cmask_sb", [128, 128], F32)
        cx.hng = sb("hng_sb", [128, NH * DH], F32)
        ident_d, cmask_d, hng_d = cx.cdram
        S.dma("sp", "consts", out=cx.ident_f[:], in_=ident_d, writes=["consts"])
        S.dma("pool", "consts2", out=cx.ident_b[:], in_=ident_d, writes=["consts"])
        S.dma("sp", "consts", out=cx.cmask[:], in_=cmask_d, writes=["consts"])
        S.dma("sp", "consts", out=cx.hng[:], in_=hng_d, writes=["consts"])
        S.op("dve", lambda e: e.memset(cx.ones1[:], 1.0), writes=["consts"])
        S.barrier()
        wout = sb("wout", [128, KC, D], BF16)
        xs = [sb(f"xs{i}", [128, KC, T], F32) for i in range(2)]
        xb = sb("xb", [128, KC, T], BF16)
        mixin = sb("mixin", [128, KC, T], BF16)
        qT = sb("qT", [64, NH, T], BF16)
        ebuf = sb("ebuf", [128, 8, T], BF16)
        rden = sb("rden", [128, T], F32)
        raw = sb("raw", [96, 2, T + 3], F32)
        carry = sb("carry", [96, 16, 3], F32)
        cacc = sb("cacc", [96, 2, T], F32)
        qkT = sb("qkT", [96, 16, T], BF16)
        vaug = sb("vaug", [128, NB, NH, DH + 1], BF16)
        sigo = sb("sigo", [128, 6, T], F32)
        g_i = sb("g_i", [4, T], F32)
        g_x = sb("g_x", [4, T], F32)
        g_t = sb("g_t", [4, T], F32)
        g_f = sb("g_f", [4, T], F32)
        g_F = sb("g_F", [4, T], F32)
        g_a = sb("g_a", [4, T], F32)
        g_M = sb("g_M", [4, T], F32)
        g_al = sb("g_al", [4, T], F32)
        g_fl = sb("g_fl", [4, T], F32)
        g_one = sb("g_one", [4, T], F32)
        Fc = sb("Fc", [4, 1], F32)
        Mc = sb("Mc", [4, 1], F32)
        Mprev = sb("Mprev", [4, NB], F32)
        dec = sb("dec", [4, NB], F32)
        decx = sb("decx", [4, NB, 4], F32)
        tokmaj = sb("tokmaj", [128, NB, 8], F32)
        decb = sb("decb", [96, NB * 4], F32)
        Cst = sb("Cst", [96, 8, DH + 1], F32)
        Cbf = sb("Cbf", [96, 8, DH + 1], BF16)
        scT = sb("scT", [128, NH, 128], BF16)
        kpr = sb("kpr", [128, NH, DH], BF16)
        hx = sb("hx", [128, NH, DH], F32)
        hn = sb("hn", [128, NH * DH], BF16)
        dn = sb("dn", [128, NH], F32)
        bst = sb("bst", [128, NH, 6], F32)
        mv = sb("mv", [128, NH, 2], F32)
        rs4 = sb("rs4", [128, NH], F32)

        Win_v = Win.rearrange("(kc p) f -> p kc f", p=128)
        for q in range(4):
            S.dma("pool", "win", out=win[:, :, q * 834:(q + 1) * 834], in_=Win_v[:, :, q * 834:(q + 1) * 834],
                  writes=["win"])
        S.dma("pool", "wout", out=wout[:], in_=Wout.rearrange("(kc p) f -> p kc f", p=128), writes=["wout"])
        kv = kv_precompute(S, cx, sb, memT, Wkv)
        S.op("dve", lambda e: e.memset(carry[:], 0.0), writes=["carry"])
        S.op("dve", lambda e: e.memset(vaug[:], 1.0), writes=["vaug"])
        S.op("dve", lambda e: e.memset(Cst[:], 0.0), writes=["Cst"])
        S.op("dve", lambda e: e.memset(Fc[:], 0.0), writes=["Fc"])
        S.op("dve", lambda e: e.memset(Mc[:], 0.0), writes=["Mc"])
        S.op("dve", lambda e: e.memset(g_one[:], 1.0), writes=["g_one"])
        psb = cx.ps[5][:].bitcast(BF16)
        src_v, dst_v = fm(src), fm(dst)
        ntile = NT // T

        def load(ti):
            sl = ti % 2
            S.dma("sp", f"xld{sl}", out=xs[sl][:], in_=src_v[:, :, ti * T:(ti + 1) * T],
                  reads=["src"], writes=[("xs", sl)])
        load(0)
        for ti in range(ntile):
            sl = ti % 2
            X = xs[sl]
            xk = ("xs", sl)
            if ti + 1 < ntile:
                load(ti + 1)
            S.op("act", lambda e: e.activation(xb[:], X[:], AF.Copy), reads=[xk], writes=["xb"])

            def proj(ps, pkey, col, m, rows=None):
                def mm(pe):
                    for c in range(KC):
                        i = pe.matmul(ps[0:m, :T], win[:, c, col:col + m], xb[:, c, :],
                                      start=(c == 0), stop=(c == KC - 1))
                    return i
                S.op("pe", mm, reads=["win", "xb"], writes=[pkey])
            for g in range(16):
                qk, h, half = g // 8, (g % 8) // 2, g % 2
                col = qk * DTOK + h * DH + half * 96
                r = g % 2
                ps = cx.ps[r]
                proj(ps, f"ps{r}", col, 96)
                S.op("act", lambda e, r=r, ps=ps: e.activation(raw[:, r, 3:3 + T], ps[0:96, :T], AF.Copy),
                     reads=[f"ps{r}"], writes=[("raw", r)])
                S.op("pool", lambda e, r=r, g=g: e.tensor_copy(raw[:, r, 0:3], carry[:, g, :]),
                     reads=["carry"], writes=[("raw", r)])
                wt = [cx.smalls[0:96, QKW + tap * 16 + g:QKW + tap * 16 + g + 1] for tap in range(4)]
                S.op("dve", lambda e, r=r, wt=wt: e.tensor_scalar(cacc[:, r, :], raw[:, r, 3:3 + T], wt[3], None, ALU.mult),
                     reads=[("raw", r), "consts"], writes=[("cacc", r)])
                for tap in (2, 1, 0):
                    S.op("dve", lambda e, r=r, wt=wt, tap=tap: e.scalar_tensor_tensor(
                        out=cacc[:, r, :], in0=raw[:, r, tap:tap + T], scalar=wt[tap], in1=cacc[:, r, :],
                        op0=ALU.mult, op1=ALU.add),
                        reads=[("raw", r), ("cacc", r), "consts"], writes=[("cacc", r)])
                S.op("pool", lambda e, r=r, g=g: e.tensor_copy(carry[:, g, :], raw[:, r, T:T + 3]),
                     reads=[("raw", r)], writes=["carry"])
                S.op("act", lambda e, r=r, g=g: e.activation(qkT[:, g, :], cacc[:, r, :], AF.Silu),
                     reads=[("cacc", r)], writes=[("qkT", g)])
            for blk in range(NB):
                for hp in range(2):
                    ps = cx.ps[hp]

                    def mm(pe, blk=blk, hp=hp, ps=ps):
                        for c in range(KC):
                            i = pe.matmul(ps[:, :2 * DH], xb[:, c, blk * 128:(blk + 1) * 128],
                                          win[:, c, 2 * DTOK + hp * 2 * DH:2 * DTOK + (hp + 1) * 2 * DH],
                                          start=(c == 0), stop=(c == KC - 1))
                        return i
                    S.op("pe", mm, reads=["win", "xb"], writes=[f"ps{hp}"])
                    S.op("dve", lambda e, blk=blk, hp=hp, ps=ps: e.tensor_copy(
                        vaug[:, blk, 2 * hp:2 * hp + 2, 0:DH], ps[:, :2 * DH].rearrange("p (h e) -> p h e", h=2)),
                        reads=[f"ps{hp}"], writes=["vaug"])
            for j in range(6):
                ps = cx.ps[j % 2]
                proj(ps, f"ps{j % 2}", 3 * DTOK + j * 128, 128)
                S.op("act", lambda e, j=j, ps=ps: e.activation(sigo[:, j, :], ps[:, :T], AF.Sigmoid),
                     reads=[f"ps{j % 2}"], writes=[("sigo", j)])
            proj(cx.ps[0], "ps0", 4 * DTOK, 4)
            proj(cx.ps[1], "ps1", 4 * DTOK + 4, 4)
            S.op("dve", lambda e: e.tensor_scalar(g_i[:], cx.ps[0][0:4, :T], cx.smalls[0:4, GB:GB + 1], None, ALU.add),
                 reads=["ps0", "consts"], writes=["g_i"])
            S.op("dve", lambda e: e.tensor_scalar(g_x[:], cx.ps[1][0:4, :T], cx.smalls[0:4, GB + 1:GB + 2], None, ALU.add),
                 reads=["ps1", "consts"], writes=["g_x"])
            S.op("dve", lambda e: e.tensor_scalar(g_t[:], g_x[:], -1.0, None, ALU.mult), reads=["g_x"], writes=["g_t"])
            S.op("dve", lambda e: e.tensor_tensor(g_t[:], g_t[:], g_x[:], ALU.max), reads=["g_x", "g_t"], writes=["g_t"])
            S.op("act", lambda e: e.activation(g_t[:], g_t[:], AF.Exp, scale=-1.0), reads=["g_t"], writes=["g_t"])
            S.op("act", lambda e: e.activation(g_t[:], g_t[:], AF.Ln, bias=1.0), reads=["g_t"], writes=["g_t"])
            S.op("dve", lambda e: e.tensor_scalar(g_f[:], g_x[:], 0.0, None, ALU.min), reads=["g_x"], writes=["g_f"])
            S.op("dve", lambda e: e.tensor_tensor(g_f[:], g_f[:], g_t[:], ALU.subtract), reads=["g_f", "g_t"], writes=["g_f"])
            S.op("dve", lambda e: e.tensor_tensor_scan(g_F[:], g_one[:], g_f[:], Fc[:, 0:1], ALU.mult, ALU.add),
                 reads=["g_f", "g_one", "Fc"], writes=["g_F"])
            S.op("dve", lambda e: e.tensor_tensor(g_a[:], g_i[:], g_F[:], ALU.subtract), reads=["g_i", "g_F"], writes=["g_a"])
            S.op("dve", lambda e: e.tensor_tensor_scan(g_M[:], g_a[:], g_a[:], Mc[:, 0:1], ALU.max, ALU.max),
                 reads=["g_a", "Mc"], writes=["g_M"])
            Mend = g_M[:].rearrange("p (b t) -> p b t", t=128)[:, :, 127:128]
            Mend_b = Mend.to_broadcast([4, NB, 128])
            S.op("dve", lambda e: e.tensor_copy(Mprev[:, 0:1], Mc[:, 0:1]), reads=["Mc"], writes=["Mprev"])
            if NB > 1:
                S.op("dve", lambda e: e.tensor_copy(Mprev[:, 1:NB].unsqueeze(2), Mend[:, 0:NB - 1, :]),
                     reads=["g_M"], writes=["Mprev"])
            S.op("dve", lambda e: e.tensor_tensor(g_al[:].rearrange("p (b t) -> p b t", t=128),
                                                  g_a[:].rearrange("p (b t) -> p b t", t=128), Mend_b, ALU.subtract),
                 reads=["g_a", "g_M"], writes=["g_al"])
            S.op("dve", lambda e: e.tensor_scalar(g_al[:], g_al[:], LOG_QSCALE, None, ALU.add), reads=["g_al"], writes=["g_al"])
            S.op("act", lambda e: e.activation(g_al[:], g_al[:], AF.Exp), reads=["g_al"], writes=["g_al"])
            S.op("dve", lambda e: e.tensor_tensor(g_fl[:].rearrange("p (b t) -> p b t", t=128),
                                                  g_F[:].rearrange("p (b t) -> p b t", t=128), Mend_b, ALU.add),
                 reads=["g_F", "g_M"], writes=["g_fl"])
            S.op("act", lambda e: e.activation(g_fl[:], g_fl[:], AF.Exp, scale=-1.0), reads=["g_fl"], writes=["g_fl"])
            S.op("dve", lambda e: e.tensor_tensor(dec[:].unsqueeze(2), Mprev[:].unsqueeze(2), Mend, ALU.subtract),
                 reads=["Mprev", "g_M"], writes=["dec"])
            S.op("act", lambda e: e.activation(dec[:], dec[:], AF.Exp), reads=["dec"], writes=["dec"])
            S.op("dve", lambda e: e.tensor_tensor(decx[:], dec[:].unsqueeze(2).to_broadcast([4, NB, 4]),
                                                  cx.ident_f[0:4, 0:4].unsqueeze(1).to_broadcast([4, NB, 4]), ALU.mult),
                 reads=["dec", "consts"], writes=["decx"])
            S.op("pe", lambda pe: pe.matmul(cx.ps[0][0:96, :NB * 4], cx.ones1[0:4, 0:96],
                                            decx[:].rearrange("p b h -> p (b h)"), start=True, stop=True),
                 reads=["decx", "consts"], writes=["ps0"])
            S.op("dve", lambda e: e.tensor_copy(decb[:], cx.ps[0][0:96, :NB * 4]), reads=["ps0"], writes=["decb"])
            for blk in range(NB):
                ps = cx.ps[1]

                def mm(pe, blk=blk, ps=ps):
                    pe.matmul(ps[:, blk * 8:blk * 8 + 4], g_al[:, blk * 128:(blk + 1) * 128], cx.ident_f[0:4, 0:4],
                              start=True, stop=True)
                    return pe.matmul(ps[:, blk * 8 + 4:blk * 8 + 8], g_fl[:, blk * 128:(blk + 1) * 128],
                                     cx.ident_f[0:4, 0:4], start=True, stop=True)
                S.op("pe", mm, reads=["g_al", "g_fl", "consts"], writes=["ps1"])
            S.op("dve", lambda e: e.tensor_copy(tokmaj[:].rearrange("p b c -> p (b c)"), cx.ps[1][:, :NB * 8]),
                 reads=["ps1"], writes=["tokmaj"])
            S.op("dve", lambda e: e.tensor_copy(Mc[:, 0:1], g_M[:, T - 1:T]), reads=["g_M", "Mprev"], writes=["Mc"])
            S.op("dve", lambda e: e.tensor_copy(Fc[:, 0:1], g_F[:, T - 1:T]), reads=["g_F"], writes=["Fc"])
            for blk in range(NB):
                tb = slice(blk * 128, (blk + 1) * 128)
                pS = cx.ps[4]
                for h in range(NH):
                    for half in range(2):
                        hh = h * 2 + half
                        S.op("act", lambda e, hh=hh, blk=blk, h=h: e.activation(
                            Cbf[:, hh, :], Cst[:, hh, :], AF.Copy, scale=decb[:, blk * 4 + h:blk * 4 + h + 1]),
                            reads=[("Cst", hh), "decb"], writes=[("Cbf", hh)])

                    def mm_s(pe, h=h, tb=tb):
                        for half in range(2):
                            i = pe.matmul(pS[:, h * 128:(h + 1) * 128], qkT[:, 8 + h * 2 + half, tb], qkT[:, h * 2 + half, tb],
                                          start=(half == 0), stop=(half == 1))
                        return i
                    S.op("pe", mm_s, reads=[("qkT", 8 + h * 2), ("qkT", 9 + h * 2), ("qkT", h * 2), ("qkT", h * 2 + 1)],
                         writes=[("pS", h)])
                    S.op("dve", lambda e, h=h, blk=blk: e.scalar_tensor_tensor(
                        out=scT[:, h, :], in0=pS[:, h * 128:(h + 1) * 128], scalar=tokmaj[:, blk, h:h + 1],
                        in1=cx.cmask[:], op0=ALU.mult, op1=ALU.mult),
                        reads=[("pS", h), "tokmaj", "consts"], writes=[("scT", h)])

                    def mm_t(pe, h=h, tb=tb):
                        for half in range(2):
                            i = pe.transpose(psb[:, h * DH + half * 96:h * DH + (half + 1) * 96],
                                             qkT[:, 8 + h * 2 + half, tb], cx.ident_b[0:96, 0:96])
                        return i
                    S.op("pe", mm_t, reads=[("qkT", 8 + h * 2), ("qkT", 9 + h * 2), "consts"], writes=[("psb", h)])
                    S.op("act", lambda e, h=h, blk=blk: e.activation(
                        kpr[:, h, :], psb[:, h * DH:(h + 1) * DH], AF.Copy, scale=tokmaj[:, blk, h:h + 1]),
                        reads=[("psb", h), "tokmaj"], writes=[("kpr", h)])
                    pX = cx.ps[2 + h // 2]
                    xo = (h % 2) * 256

                    def mm_x(pe, h=h, tb=tb, blk=blk, pX=pX, xo=xo):
                        pe.matmul(pX[:, xo:xo + DH + 1], scT[:, h, :], vaug[:, blk, h, :], start=True, stop=False)
                        pe.matmul(pX[:, xo:xo + DH + 1], qkT[:, h * 2, tb], Cbf[:, h * 2, :], start=False, stop=False)
                        return pe.matmul(pX[:, xo:xo + DH + 1], qkT[:, h * 2 + 1, tb], Cbf[:, h * 2 + 1, :],
                                         start=False, stop=True)
                    S.op("pe", mm_x, reads=[("scT", h), "vaug", ("qkT", h * 2), ("qkT", h * 2 + 1),
                                            ("Cbf", h * 2), ("Cbf", h * 2 + 1)], writes=[("pX", h)])
                    pU = cx.ps[6 + h % 2]

                    def mm_u(pe, h=h, blk=blk, pU=pU):
                        for half in range(2):
                            i = pe.matmul(pU[0:96, half * 256:half * 256 + DH + 1], kpr[:, h, half * 96:(half + 1) * 96],
                                          vaug[:, blk, h, :], start=True, stop=True)
                        return i
                    S.op("pe", mm_u, reads=[("kpr", h), "vaug"], writes=[f"ps{6 + h % 2}"])
                    for half in range(2):
                        hh = h * 2 + half
                        S.op("dve", lambda e, hh=hh, half=half, blk=blk, h=h, pU=pU: e.scalar_tensor_tensor(
                            out=Cst[:, hh, :], in0=Cst[:, hh, :], scalar=decb[:, blk * 4 + h:blk * 4 + h + 1],
                            in1=pU[0:96, half * 256:half * 256 + DH + 1], op0=ALU.mult, op1=ALU.add),
                            reads=[("Cst", hh), "decb", f"ps{6 + h % 2}"], writes=[("Cst", hh)])
                    S.op("dve", lambda e, h=h, blk=blk, pX=pX, xo=xo: e.tensor_scalar(
                        dn[:, h:h + 1], pX[:, xo + DH:xo + DH + 1], -1.0, tokmaj[:, blk, 4 + h:5 + h], ALU.mult, ALU.max),
                        reads=[("pX", h), "tokmaj"], writes=[("dn", h)])
                    S.op("dve", lambda e, h=h, pX=pX, xo=xo: e.tensor_tensor(
                        dn[:, h:h + 1], dn[:, h:h + 1], pX[:, xo + DH:xo + DH + 1], ALU.max),
                        reads=[("pX", h), ("dn", h)], writes=[("dn", h)])
                    S.op("dve", lambda e, h=h: e.reciprocal(dn[:, h:h + 1], dn[:, h:h + 1]),
                         reads=[("dn", h)], writes=[("dn", h)])
                    S.op("act", lambda e, h=h, pX=pX, xo=xo: e.activation(hx[:, h, :], pX[:, xo:xo + DH], AF.Copy,
                                                                         scale=dn[:, h:h + 1]),
                         reads=[("pX", h), ("dn", h)], writes=[("hx", h)])
                    S.op("dve", lambda e, h=h: e.bn_stats(bst[:, h, :], hx[:, h, :]), reads=[("hx", h)], writes=[("bst", h)])
                    S.op("dve", lambda e, h=h: e.bn_aggr(mv[:, h, :], bst[:, h, :]), reads=[("bst", h)], writes=[("mv", h)])
                mvk = [("mv", h) for h in range(NH)]
                S.op("dve", lambda e: e.tensor_scalar(rs4[:].unsqueeze(2), mv[:, :, 1:2], EPS, None, ALU.add),
                     reads=mvk, writes=["rs4"])
                S.op("pool", lambda e: e.tensor_tensor(rs4[:], rs4[:], cx.mhalf[:, 0:NH], ALU.pow),
                     reads=["rs4", "consts"], writes=["rs4"])
                for h in range(NH):
                    S.op("dve", lambda e, h=h: e.tensor_scalar(hx[:, h, :], hx[:, h, :], mv[:, h, 0:1], rs4[:, h:h + 1],
                                                               ALU.subtract, ALU.mult),
                         reads=[("hx", h), ("mv", h), "rs4"], writes=[("hx", h)])
                S.op("dve", lambda e: e.tensor_tensor(hn[:], hx[:].rearrange("p h e -> p (h e)"), cx.hng[:], ALU.mult),
                     reads=[("hx", h) for h in range(NH)] + ["consts"], writes=["hn"])

                def mm_tr(pe):
                    for j in range(6):
                        i = pe.transpose(psb[:, j * 128:(j + 1) * 128], hn[:, j * 128:(j + 1) * 128], cx.ident_b[:])
                    return i
                S.op("pe", mm_tr, reads=["hn", "consts"], writes=[("psb", h) for h in range(NH)])
                S.op("dve", lambda e, tb=tb: e.tensor_tensor(
                    mixin[:, 0:6, tb], psb[:, 0:768].rearrange("p (j t) -> p j t", t=128), sigo[:, :, tb], ALU.mult),
                    reads=[("psb", h) for h in range(NH)] + [("sigo", j) for j in range(6)],
                    writes=[("mixin", j) for j in range(6)])
            xattn_out_ln(S, cx, X, xk, xb, win, 4 * DTOK + 8, wout, mixin, qT, ebuf, rden, kv, T, gcol, bcol)
            S.dma("sp", f"xst{sl}", out=dst_v[:, :, ti * T:(ti + 1) * T], in_=X[:],
                  reads=[xk], writes=["dst"])
        S.barrier()


def build_program(NT, stages=("f00",), debug=False, T_FFN=512, T_MIX=256, FFN_NXS=2):
    nc = bass.Bass("TRN2", target_bir_lowering=False)
    cx = Ctx()
    cx.nc = nc

    def din(name, shape):
        return nc.dram_tensor(name, shape, F32, kind="ExternalInput").ap()
    xT = din("xT", [D, NT])
    memT = din("memT", [D, NMEM])
    smalls_d = din("smalls", [128, 256])
    ident_d = din("ident", [128, 128])
    cmask_d = din("cmask", [128, 128])
    hng_d = din("hng", [128, NH * DH])
    w = {}
    w["ffn_w_gate"] = din("ffn_w_gate", [2, 2, D, DFF])
    w["ffn_w_up"] = din("ffn_w_up", [2, 2, D, DFF])
    w["ffn_w_down"] = din("ffn_w_down", [2, 2, DFF, D])
    w["w_kv_mem"] = din("w_kv_mem", [2, D, 2 * DXA])
    w["w_out"] = din("w_out", [2, D, D])
    w["w_in_conv"] = din("w_in_conv", [1, D, D_IN_CONV])
    w["w_in_mlstm"] = din("w_in_mlstm", [1, D, D_IN_ML])
    outT = nc.dram_tensor("outT", [D, NT], F32, kind="ExternalOutput").ap()
    nst = len(stages)
    bufs = []
    if debug:
        for i in range(nst - 1):
            bufs.append(nc.dram_tensor(f"act{i}", [D, NT], F32, kind="ExternalOutput").ap())
    else:
        pp = [nc.dram_tensor(f"scr{i}", [D, NT], F32, kind="Internal").ap() for i in range(2)]
        for i in range(nst - 1):
            bufs.append(pp[i % 2])
    bufs.append(outT)

    with ExitStack() as es:
        S = Sched(nc, es)
        cx.ps = [es.enter_context(nc.psum_tensor(f"ps{i}", [128, 512], F32)) for i in range(8)]
        cx.smalls = es.enter_context(nc.sbuf_tensor("smalls_sb", [128, 256], F32))
        cx.ones_f = es.enter_context(nc.sbuf_tensor("ones_f", [128, 128], F32))
        cx.mhalf = es.enter_context(nc.sbuf_tensor("mhalf", [128, 512], F32))
        cx.ln_zsq = None
        cx.ffn_nxs = FFN_NXS
        cx.ln_mean = es.enter_context(nc.sbuf_tensor("ln_mean", [128, 512], F32))
        cx.ln_var = es.enter_context(nc.sbuf_tensor("ln_var", [128, 512], F32))
        cx.cdram = (ident_d, cmask_d, hng_d)
        S.dma("sp", "consts", out=cx.smalls[:], in_=smalls_d, writes=["consts"])
        S.op("dve", lambda e: e.memset(cx.ones_f[:], 1.0 / D), writes=["consts"])
        S.op("dve", lambda e: e.memset(cx.mhalf[:], -0.5), writes=["consts"])
        S.barrier()
        cur = xT
        for si, st in enumerate(stages):
            dst = bufs[si]
            kind, l, i = st[0], int(st[1]), int(st[2]) if len(st) > 2 else 0
            if kind == "f":
                col = (l * 3 + (0 if i == 0 else 2)) * 16
                ffn_stage(S, cx, cur, dst, w["ffn_w_gate"][l, i], w["ffn_w_up"][l, i], w["ffn_w_down"][l, i],
                          col, col + 8, NT, T_FFN, si)
            elif kind == "m" and l == 0:
                col = (l * 3 + 1) * 16
                mix0_stage(S, cx, cur, dst, w["w_in_conv"][0], w["w_out"][0], w["w_kv_mem"][0], memT,
                           col, col + 8, NT, T_MIX, si)
            elif kind == "m" and l == 1:
                col = (l * 3 + 1) * 16
                mix1_stage(S, cx, cur, dst, w["w_in_mlstm"][0], w["w_out"][1], w["w_kv_mem"][1], memT,
                           col, col + 8, NT, T_MIX, si)
            cur = dst
        S.barrier()
    return nc


def make_smalls(inputs):
    sm = np.zeros((128, 256), np.float32)
    ln_g, ln_b = np.asarray(inputs["ln_g"]), np.asarray(inputs["ln_b"])
    for l in range(2):
        for i in range(3):
            c = (l * 3 + i) * 16
            sm[:, c:c + 8] = ln_g[l, i].reshape(8, 128).T
            sm[:, c + 8:c + 16] = ln_b[l, i].reshape(8, 128).T
    cw = np.asarray(inputs["conv_w"])[0]
    for tap in range(3):
        sm[:, 96 + tap * 6:96 + tap * 6 + 6] = cw[tap].reshape(6, 128).T
    qw = np.asarray(inputs["qk_conv_w"])[0]
    for tap in range(4):
        for g in range(16):
            qk, h, half = g // 8, (g % 8) // 2, g % 2
            c0 = qk * DTOK + h * DH + half * 96
            sm[0:96, QKW + tap * 16 + g] = qw[tap, c0:c0 + 96]
    bg = np.asarray(inputs["b_gates"])[0]
    sm[0:4, GB] = bg[0:4]
    sm[0:4, GB + 1] = bg[4:8]
    return sm


WNAMES = ["ffn_w_gate", "ffn_w_up", "ffn_w_down", "w_kv_mem", "w_out", "w_in_conv", "w_in_mlstm"]


def make_inmap(inputs, b, NT, shared=None):
    if shared is None:
        shared = {k: np.ascontiguousarray(np.asarray(inputs[k], dtype=np.float32)) for k in WNAMES}
        shared["smalls"] = make_smalls(inputs)
        shared["ident"] = np.eye(128, dtype=np.float32)
        shared["cmask"] = np.triu(np.ones((128, 128), np.float32))
        shared["hng"] = np.ascontiguousarray(np.broadcast_to(
            np.asarray(inputs["head_norm_g"], np.float32).reshape(1, NH * DH), (128, NH * DH)))
    m = dict(shared)
    m["xT"] = np.ascontiguousarray(np.asarray(inputs["x"])[b, :NT].T)
    m["memT"] = np.ascontiguousarray(np.asarray(inputs["mem"])[b].T)
    return m


ALL_STAGES = ("f00", "m0", "f01", "f10", "m1", "f11")


def kernel(**inputs):
    x = np.asarray(inputs["x"])
    B, NT, _ = x.shape
    nc = build_program(NT, stages=ALL_STAGES)
    shared = None
    in_maps = []
    for b in range(B):
        m = make_inmap(inputs, b, NT, shared)
        if shared is None:
            shared = {k: v for k, v in m.items() if k not in ("xT", "memT")}
        in_maps.append(m)
    res = run_bass_kernel_spmd(nc, in_maps, core_ids=list(range(B)))
    out = np.stack([np.ascontiguousarray(np.asarray(r["outT"]).T) for r in res.results])
    return out.astype(np.float32)
```
